# Optimizing a Trainium2 kernel written in Bass

```python
import math
import jax, jax.numpy as jnp
from jax import lax
import numpy as np

D_MODEL = 1024
BATCH = 4
SEQ = 8192
DEPTH = 2

CHUNK = 64
MEM_LEN = 256
POOL_GROUPS = 4
POOL_WINDOWS = (2, 4, 8, 16)
POOL_WIDTH = D_MODEL // 2
POOL_GC = POOL_WIDTH // POOL_GROUPS
N_HEADS = 8
HEAD_DIM = 64
ATTN_WIDTH = N_HEADS * HEAD_DIM
IDX_HEADS = 8
IDX_DIM = 64
TOPK_MAX = 256
Q_BLOCK = 128
ATTN_SCALE = HEAD_DIM ** -0.5
IDX_W_SCALE = (IDX_DIM ** -0.5) * (IDX_HEADS ** -0.5)
MEM_HEADS = 4
MEM_HEAD_DIM = 128
MEM_WIDTH = MEM_HEADS * MEM_HEAD_DIM
MEM_SCALE = MEM_HEAD_DIM ** -0.5
N_BRANCH = 3
REL_BUCKETS = 32
REL_MAX_DIST = 1024
EPS = 1e-6
NEG = -1e30
IN_SPLITS = (POOL_WIDTH, POOL_WIDTH,
             ATTN_WIDTH, ATTN_WIDTH, ATTN_WIDTH, ATTN_WIDTH,
             IDX_HEADS * IDX_DIM, IDX_DIM, IDX_HEADS,
             MEM_WIDTH, MEM_WIDTH,
             N_BRANCH * D_MODEL)
IN_COLS = sum(IN_SPLITS)

kernel_name = "hybrid_pool_dsa_mem_gated_trunk"


def rms_norm(x, g):
    x32 = x.astype(jnp.float32)
    y = x32 * lax.rsqrt(jnp.mean(x32 * x32, axis=-1, keepdims=True) + EPS)
    return (y * g.astype(jnp.float32)).astype(x.dtype)


def rel_bucket(rel):
    half = REL_BUCKETS // 2
    max_exact = half // 2
    ret = jnp.where(rel > 0, half, 0)
    n = jnp.abs(rel)
    nf = jnp.maximum(n, 1).astype(jnp.float32)
    large = max_exact + (jnp.log(nf / max_exact) / math.log(REL_MAX_DIST / max_exact)
                         * (half - max_exact)).astype(jnp.int32)
    large = jnp.minimum(large, half - 1)
    return ret + jnp.where(n < max_exact, n, large)


def multi_scale_pool(u, pool_w, pool_scale):
    b, s, _ = u.shape
    ug = u.reshape(b, s, POOL_GROUPS, POOL_GC).astype(jnp.float32)
    cs = jnp.cumsum(ug, axis=1)
    cs = jnp.concatenate([jnp.zeros_like(cs[:, :1]), cs], axis=1)
    t = jnp.arange(s, dtype=jnp.int32)[:, None]
    win = jnp.asarray(POOL_WINDOWS, jnp.int32)[None, :]
    start = jnp.maximum(t + 1 - win, 0)
    grp = jnp.arange(POOL_GROUPS, dtype=jnp.int32)[None, :]
    lo = cs[:, start, grp, :]
    count = (t + 1 - start).astype(jnp.float32)[None, :, :, None]
    pooled = (cs[:, 1:] - lo) / count - ug
    mixed = jnp.einsum('bsgc,gcd->bsgd', pooled.astype(u.dtype), pool_w)
    return mixed.reshape(b, s, POOL_WIDTH) * pool_scale


def dsa_attention(q, k, v, iq, ik, iw, rel_bias):
    b, s = q.shape[:2]
    topk = min(TOPK_MAX, s // 4)
    nb = s // Q_BLOCK
    key_pos = jnp.arange(s, dtype=jnp.int32)
    ik32 = ik.astype(jnp.float32)
    gather = jax.vmap(lambda arr, idx: arr[idx])

    def to_blocks(a):
        return jnp.moveaxis(a.reshape((b, nb, Q_BLOCK) + a.shape[2:]), 1, 0)

    def block(args):
        qb, iqb, iwb, qpos = args
        limit = (qpos // CHUNK + 1) * CHUNK
        sc = jnp.einsum('bqhd,bsd->bqhs', iqb.astype(jnp.float32), ik32)
        index_score = jnp.einsum('bqhs,bqh->bqs', jax.nn.relu(sc),
                                 iwb.astype(jnp.float32) * IDX_W_SCALE)
        admissible = key_pos[None, :] < limit[:, None]
        index_score = jnp.where(admissible[None], index_score, NEG)
        _, idx = lax.top_k(index_score, topk)
        k_sel = gather(k, idx)
        v_sel = gather(v, idx)
        logits = jnp.einsum('bqhd,bqkhd->bqhk', qb, k_sel).astype(jnp.float32) * ATTN_SCALE
        bias = rel_bias[rel_bucket(idx - qpos[None, :, None])]
        logits = logits + jnp.moveaxis(bias, -1, 2).astype(jnp.float32)
        valid = idx < limit[None, :, None]
        logits = jnp.where(valid[:, :, None, :], logits, NEG)
        p = jax.nn.softmax(logits, axis=-1).astype(v.dtype)
        return jnp.einsum('bqhk,bqkhd->bqhd', p, v_sel)

    out = lax.map(block, (to_blocks(q), to_blocks(iq), to_blocks(iw),
                          key_pos.reshape(nb, Q_BLOCK)))
    return jnp.moveaxis(out, 0, 1).reshape(b, s, N_HEADS * HEAD_DIM)


def memory_attention(q, mem_k, mem_v):
    b, s = q.shape[:2]
    logits = jnp.einsum('bshd,bmhd->bhsm', q, mem_k).astype(jnp.float32) * MEM_SCALE
    p = jax.nn.softmax(logits, axis=-1).astype(mem_v.dtype)
    return jnp.einsum('bhsm,bmhd->bshd', p, mem_v).reshape(b, s, MEM_WIDTH)


def setup_inputs(seed: int = 0) -> dict:
    key = jax.random.key(seed)
    ks = jax.random.split(key, 14)
    f32 = jnp.float32
    nrm = lambda k, shape, scale: jax.random.normal(k, shape, f32) * scale
    return {
        "x": nrm(ks[0], (BATCH, SEQ, D_MODEL), 1.0),
        "mem": nrm(ks[1], (BATCH, MEM_LEN, D_MODEL), 1.0),
        "norm_g": 1.0 + nrm(ks[2], (DEPTH, D_MODEL), 0.02),
        "w_in": nrm(ks[3], (DEPTH, D_MODEL, IN_COLS), D_MODEL ** -0.5),
        "pool_w": nrm(ks[4], (DEPTH, POOL_GROUPS, POOL_GC, POOL_GC), POOL_GC ** -0.5),
        "pool_scale": 1.0 + nrm(ks[5], (DEPTH, POOL_WIDTH), 0.02),
        "mem_norm_g": 1.0 + nrm(ks[6], (DEPTH, D_MODEL), 0.02),
        "w_mem_kv": nrm(ks[7], (DEPTH, D_MODEL, 2 * MEM_WIDTH), D_MODEL ** -0.5),
        "w_branch": nrm(ks[8], (DEPTH, N_BRANCH, POOL_WIDTH, D_MODEL), POOL_WIDTH ** -0.5),
        "w_out": nrm(ks[9], (DEPTH, D_MODEL, D_MODEL), D_MODEL ** -0.5),
        "rel_bias": nrm(ks[10], (REL_BUCKETS, N_HEADS), 0.5),
        "final_g": 1.0 + nrm(ks[11], (D_MODEL,), 0.02),
    }


def reference(x, mem, norm_g, w_in, pool_w, pool_scale, mem_norm_g, w_mem_kv,
              w_branch, w_out, rel_bias, final_g):
    b, s, _ = x.shape
    split_points = np.cumsum(np.asarray(IN_SPLITS))[:-1].tolist()
    for l in range(DEPTH):
        h = rms_norm(x, norm_g[l])
        proj = h @ w_in[l]
        (pool_u, pool_z, aq, ak, av, az, iq, ik, iw, mq, mz, gates) = jnp.split(
            proj, split_points, axis=-1)

        y_pool = multi_scale_pool(pool_u, pool_w[l], pool_scale[l]) * jax.nn.silu(pool_z)

        y_attn = dsa_attention(
            aq.reshape(b, s, N_HEADS, HEAD_DIM), ak.reshape(b, s, N_HEADS, HEAD_DIM),
            av.reshape(b, s, N_HEADS, HEAD_DIM), iq.reshape(b, s, IDX_HEADS, IDX_DIM),
            ik, iw, rel_bias) * jax.nn.silu(az)

        mkv = rms_norm(mem, mem_norm_g[l]) @ w_mem_kv[l]
        mk, mv = jnp.split(mkv, 2, axis=-1)
        mm = mem.shape[1]
        y_mem = memory_attention(
            mq.reshape(b, s, MEM_HEADS, MEM_HEAD_DIM),
            mk.reshape(b, mm, MEM_HEADS, MEM_HEAD_DIM),
            mv.reshape(b, mm, MEM_HEADS, MEM_HEAD_DIM)) * jax.nn.silu(mz)

        g = jax.nn.sigmoid(gates).reshape(b, s, N_BRANCH, D_MODEL)
        merged = (g[:, :, 0] * (y_pool @ w_branch[l, 0])
                  + g[:, :, 1] * (y_attn @ w_branch[l, 1])
                  + g[:, :, 2] * (y_mem @ w_branch[l, 2]))
        x = x + merged @ w_out[l]
    return rms_norm(x, final_g)
```

```python
import itertools
import math
import os
import numpy as np
import ml_dtypes
from contextlib import ExitStack
import concourse.bass as bass
import concourse.mybir as mybir
from concourse.bass_utils import run_bass_kernel_spmd

F32 = mybir.dt.float32
BF16 = mybir.dt.bfloat16
AF = mybir.ActivationFunctionType
ALU = mybir.AluOpType
AX = mybir.AxisListType

D = 1024
NT = 8192
NB = 64
NSLOT = 32
NOWN = 4096
INC = 7752
EPS = 1e-6
NEG = -1e30
ATTN_SCALE = 64 ** -0.5
MEM_SCALE = 128 ** -0.5
IDX_W_SCALE = (64 ** -0.5) * (8 ** -0.5)
NBIS = 14
TOPK = 256
C_PU, C_PZ, C_AQ, C_AK, C_AV, C_AZ, C_IQ, C_IK, C_IW, C_MQ, C_MZ, C_G = (
    0, 512, 1024, 1536, 2048, 2560, 3072, 3584, 3648, 3656, 4168, 4680)
NNEAR = 7


class Res:
    __slots__ = ("name", "w", "r", "excl")

    def __init__(self, name="", excl=False):
        self.name = name
        self.w = None
        self.r = {}
        self.excl = excl


class DmaSem:
    __slots__ = ("key", "sem", "count")

    def __init__(self, key, sem):
        self.key = key
        self.sem = sem
        self.count = 0


class Sched:
    ENGS = ("pe", "act", "dve", "pool", "sp")

    def __init__(self, nc, stack):
        self.nc = nc
        self.stack = stack
        self.sems = {}
        self.tick = {}
        self.waited = {}
        self.ops = {}
        self.dmasems = []
        self.free = []
        self.epoch = -1
        self.nins = 0
        for e in self.ENGS:
            self.ops[e] = []
        self._new_epoch()

    def _ekey(self, e):
        return "E_%s_%d" % (e, self.epoch)

    def _new_epoch(self):
        self.epoch += 1
        for e in self.ENGS:
            phys = "P_%s_%d" % (e, self.epoch % 3)
            if phys not in self.sems:
                self.sems[phys] = self.stack.enter_context(self.nc.semaphore("sem_%s_%d" % (e, self.epoch % 3)))
            self.sems[self._ekey(e)] = self.sems[phys]
            self.tick[e] = 0
            self.waited[e] = {}
            if self.epoch >= 2:
                nxt = "P_%s_%d" % (e, (self.epoch + 1) % 3)
                self.ops[e].append(((), ("clear", nxt), None))

    def dma_sem(self):
        if self.free:
            return self.free.pop()
        key = "D_%d" % len(self.dmasems)
        sem = self.stack.enter_context(self.nc.semaphore("dsem_%d" % len(self.dmasems)))
        self.sems[key] = sem
        d = DmaSem(key, sem)
        self.dmasems.append(d)
        return d

    def _need(self, eng, waits, ev, same_ok):
        if ev is None:
            return
        key, val, src = ev
        if key.startswith("E_") and not key.endswith("_%d" % self.epoch):
            return
        if src == eng and not same_ok:
            return
        if self.waited[eng].get(key, 0) >= val:
            return
        if waits.get(key, 0) < val:
            waits[key] = val

    def op(self, eng, fn, reads=(), writes=(), dma=None):
        waits = {}
        raw_same = (eng != "pe")
        for R in reads:
            self._need(eng, waits, R.w, raw_same)
            if R.excl:
                for key, (val, src) in R.r.items():
                    self._need(eng, waits, (key, val, src), False)
        for R in writes:
            if R.w is not None and not (dma is not None and R.w[0] == dma.key):
                self._need(eng, waits, R.w, False)
            for key, (val, src) in R.r.items():
                self._need(eng, waits, (key, val, src), False)
        for key, val in waits.items():
            self.waited[eng][key] = val
        if dma is None:
            self.tick[eng] += 1
            ev = (self._ekey(eng), self.tick[eng], eng)
            inc = (self._ekey(eng), 1)
        else:
            dma.count += 16
            ev = (dma.key, dma.count, "dma")
            inc = (dma.key, 16)
        self.ops[eng].append((tuple(waits.items()), fn, inc))
        self.nins += 1
        for R in reads:
            old = R.r.get(ev[0])
            if old is None or old[0] < ev[1]:
                R.r[ev[0]] = (ev[1], ev[2])
        for R in writes:
            R.w = ev
            R.r = {}
        return ev

    def barrier(self):
        for eng in self.ENGS:
            waits = {}
            for e2 in self.ENGS:
                if e2 != eng and self.tick[e2] > 0:
                    self._need(eng, waits, (self._ekey(e2), self.tick[e2], e2), True)
            for d in self.dmasems:
                if d.count > 0:
                    self._need(eng, waits, (d.key, d.count, "dma"), True)
            self.ops[eng].append((tuple(waits.items()), None, None))
        self._new_epoch()
        for eng in self.ENGS:
            for d in self.dmasems:
                if d.count > 0:
                    self.waited[eng][d.key] = d.count

    def emit(self):
        nc = self.nc
        sems = self.sems

        def run(handle, ops):
            for waits, fn, inc in ops:
                for key, val in waits:
                    handle.wait_ge(sems[key], val)
                if isinstance(fn, tuple):
                    handle.sem_clear(sems[fn[1]])
                elif fn is not None:
                    ins = fn(handle)
                    ins.then_inc(sems[inc[0]], inc[1])

        ops = self.ops
        with nc.Block() as block:
            @block.sync
            def _(h):
                run(h, ops["sp"])

            @block.tensor
            def _(h):
                run(h, ops["pe"])

            @block.scalar
            def _(h):
                run(h, ops["act"])

            @block.vector
            def _(h):
                run(h, ops["dve"])

            @block.gpsimd
            def _(h):
                run(h, ops["pool"])
        self.ops = {e: [] for e in self.ENGS}


class Buf:
    __slots__ = ("t", "res", "sem")

    def __init__(self, t, res, sem=None):
        self.t = t
        self.res = res
        self.sem = sem


class Ctx:
    def __init__(self, nc, S):
        self.nc = nc
        self.S = S
        self.uid = 0
        self.rr = 0
        self.recycle = False
        self.phase_sems = []

    def sb(self, st, shape, dt, dma=False, name=None):
        self.uid += 1
        t = st.enter_context(self.nc.sbuf_tensor("%s_%d" % (name or "sb", self.uid), list(shape), dt))
        sem = self.S.dma_sem() if dma else None
        if sem is not None and self.recycle:
            self.phase_sems.append(sem)
        return Buf(t, Res(name or "sb"), sem)

    def end_phase(self):
        self.S.free.extend(self.phase_sems)
        self.phase_sems = []

    def ps(self, st, shape, dt, name=None):
        self.uid += 1
        t = st.enter_context(self.nc.psum_tensor("%s_%d" % (name or "ps", self.uid), list(shape), dt))
        return Buf(t, Res(name or "ps", excl=True))

    def mm(self, out, lhsT, rhs, start, stop, reads, writes, nocheck=False):
        kw = {"skip_group_check": True} if nocheck else {}
        self.S.op("pe", lambda e: e.matmul(out=out, lhsT=lhsT, rhs=rhs, start=start, stop=stop, **kw),
                  [b.res for b in reads], [b.res for b in writes])

    def tr(self, out, in_, ident, reads, writes):
        self.S.op("pe", lambda e: e.transpose(out=out, in_=in_, identity=ident),
                  [b.res for b in reads], [b.res for b in writes])

    def act(self, out, in_, func, reads, writes, scale=None, bias=None, accum_out=None):
        kw = {}
        if scale is not None:
            kw["scale"] = scale
        if bias is not None:
            kw["bias"] = bias
        if accum_out is not None:
            kw["accum_out"] = accum_out
        self.S.op("act", lambda e: e.activation(out=out, in_=in_, func=func, **kw),
                  [b.res for b in reads], [b.res for b in writes])

    def ts(self, eng, out, in0, s1, s2, op0, op1, reads, writes, accum_out=None):
        kw = {}
        if op1 is not None:
            kw["op1"] = op1
        if accum_out is not None:
            kw["accum_out"] = accum_out
        self.S.op(eng, lambda e: e.tensor_scalar(out=out, in0=in0, scalar1=s1, scalar2=s2, op0=op0, **kw),
                  [b.res for b in reads], [b.res for b in writes])

    def tt(self, eng, out, in0, in1, op, reads, writes):
        self.S.op(eng, lambda e: e.tensor_tensor(out=out, in0=in0, in1=in1, op=op),
                  [b.res for b in reads], [b.res for b in writes])

    def stt(self, out, in0, scalar, in1, op0, op1, reads, writes):
        self.S.op("dve", lambda e: e.scalar_tensor_tensor(out=out, in0=in0, scalar=scalar, in1=in1, op0=op0, op1=op1),
                  [b.res for b in reads], [b.res for b in writes])

    def cp(self, eng, out, in_, reads, writes):
        if eng == "act":
            self.S.op("act", lambda e: e.copy(out=out, in_=in_), [b.res for b in reads], [b.res for b in writes])
        else:
            self.S.op(eng, lambda e: e.tensor_copy(out=out, in_=in_), [b.res for b in reads], [b.res for b in writes])

    def cp_rr(self, out, in_, reads, writes):
        self.rr += 1
        self.cp("act" if self.rr % 2 else "dve", out, in_, reads, writes)

    def memset(self, eng, out, val, writes):
        self.S.op(eng, lambda e: e.memset(out, val), [], [b.res for b in writes])

    def recip(self, out, in_, reads, writes):
        self.S.op("dve", lambda e: e.reciprocal(out=out, in_=in_), [b.res for b in reads], [b.res for b in writes])

    def reduce(self, out, in_, op, reads, writes, absv=False):
        kw = {"apply_absolute_value": True} if absv else {}
        self.S.op("dve", lambda e: e.tensor_reduce(out=out, in_=in_, axis=AX.X, op=op, **kw),
                  [b.res for b in reads], [b.res for b in writes])

    def load(self, out, in_, dst, src_res=()):
        self.S.op("sp", lambda e: e.dma_start(out=out, in_=in_), list(src_res), [dst.res], dma=dst.sem)

    def store(self, out, in_, src, dst_res):
        self.S.op("pool", lambda e: e.dma_start(out=out, in_=in_), [src.res], list(dst_res), dma=src.sem)


def _rel_bucket_np(rel):
    half = 16
    max_exact = 8
    ret = np.where(rel > 0, half, 0)
    n = np.abs(rel)
    nf = np.maximum(n, 1).astype(np.float32)
    large = max_exact + (np.log(nf / np.float32(max_exact)) / np.float32(math.log(1024 / max_exact))
                         * np.float32(half - max_exact)).astype(np.int32)
    large = np.minimum(large, half - 1)
    return ret + np.where(n < max_exact, n, large)


def _tile_info(d):
    s = np.arange(128)[:, None] + 128 * d
    t = np.arange(128)[None, :]
    rel = (s - t).astype(np.int32)
    limit = (t // 64 + 1) * 64
    adm = s < limit
    return _rel_bucket_np(rel), adm


_PAIRS = None


def _pairs():
    global _PAIRS
    if _PAIRS is None:
        pairs = []
        for n in range(NNEAR):
            bs = set()
            for j in (0, 1):
                d = 1 - n - j
                bk, adm = _tile_info(d)
                bs |= set(np.unique(bk[adm]).tolist())
            for b in sorted(bs):
                pairs.append((n, int(b)))
        _PAIRS = pairs
    return _PAIRS


def _tables(j):
    pairs = _pairs()
    ind = np.zeros((128, len(pairs), 128), np.float32)
    for pi, (n, b) in enumerate(pairs):
        d = 1 - n - j
        bk, adm = _tile_info(d)
        ind[:, pi, :] = ((bk == b) & adm)
    pen = np.zeros((128, 256), np.float32)
    for c, n in ((0, 1), (1, 0)):
        d = 1 - n - j
        _, adm = _tile_info(d)
        pen[:, c * 128:(c + 1) * 128] = np.where(adm.T, 0.0, NEG)
    invc = np.zeros((128, 4, 128), np.float32)
    tpos = np.arange(128) + 128 * j
    for g, w in enumerate((2, 4, 8, 16)):
        invc[:, g, :] = (1.0 / np.minimum(w, tpos + 1))[None, :]
    return ind.astype(ml_dtypes.bfloat16), pen, invc


def build_program(debug=False, phases=('pro', 'K', 'P', 'A1', 'A2')):
    nc = bass.Bass("TRN2", target_bir_lowering=False)
    npairs = len(_pairs())

    def din(name, shape, dt=F32):
        return nc.dram_tensor(name, list(shape), dt, kind="ExternalInput").ap()

    def dscr(name, shape, dt):
        return nc.dram_tensor(name, list(shape), dt, kind="ExternalOutput" if (debug and name in debug) else "Internal").ap()

    xfull = din("xfull", [NT, D])
    LW = []
    for l in range(2):
        LW.append(dict(
            w_in=din("w_in%d" % l, [D, INC]), norm_g=din("norm_g%d" % l, [128, 8]),
            pool_w=din("pool_w%d" % l, [4, 128, 128]), pool_scale=din("pool_scale%d" % l, [128, 4]),
            mem_g=din("mem_g%d" % l, [128, 8]), w_mkv=din("w_mkv%d" % l, [D, 1024]),
            w_br=din("w_br%d" % l, [3, 512, D]), w_out=din("w_out%d" % l, [D, D])))
    mem = din("mem", [256, D])
    relb = din("relb", [1, 256])
    fin_g = din("fin_g", [1, D])
    ident_d = din("ident", [128, 128], BF16)
    TAB = {}
    for tb in ("0", "1", "o"):
        TAB[tb] = dict(ind_d=din("ind" + tb, [128, npairs, 128], BF16), pen_d=din("pen" + tb, [128, 256]),
                       invc_d=din("invc" + tb, [128, 4, 128]), blend_d=din("blend" + tb, [128, 2]))
    x1full = nc.dram_tensor("x1full", [NT, D], F32, kind="Internal").ap()
    out_d = nc.dram_tensor("out", [NOWN, D], F32, kind="ExternalOutput").ap()

    k_scr = dscr("k_scr", [128, 4, NT], BF16)
    ik_scr = dscr("ik_scr", [128, NT], BF16)
    v_scr = dscr("v_scr", [NT, 8 * 65], BF16)
    u_scr = dscr("u_scr", [128, 4, 16 + NT], F32)
    q_scr = dscr("q_scr", [128, 4, NOWN], BF16)
    iq_scr = dscr("iq_scr", [128, 4, NOWN], BF16)
    mq_scr = dscr("mq_scr", [128, 4, NOWN], BF16)
    sz_scr = dscr("sz_scr", [128, 4, NOWN], F32)
    saz_scr = dscr("saz_scr", [NOWN, 512], F32)
    smz_scr = dscr("smz_scr", [NOWN, 512], F32)
    sg_scr = dscr("sg_scr", [NOWN, 3072], F32)
    iw_scr = dscr("iw_scr", [NOWN, 8], F32)
    xo_scr = dscr("xo_scr", [NOWN, D], F32)
    y_scr = dscr("y_scr", [128, 3, 4, NOWN], BF16)

    with ExitStack() as top:
        S = Sched(nc, top)
        C = Ctx(nc, S)
        scr_res = {}

        def sres(name, idx):
            k = (name, idx)
            if k not in scr_res:
                scr_res[k] = Res(name)
            return scr_res[k]

        ident = C.sb(top, [128, 128], BF16, dma=True, name="ident")
        EM = C.sb(top, [128, NNEAR, 8, 128], BF16, name="EM")
        mkT = C.sb(top, [128, 4, 256], BF16, name="mkT")
        mv = C.sb(top, [128, 2, 4, 129], BF16, name="mv")
        poolW = C.sb(top, [128, 4, 128], BF16, name="poolW")
        pscale = C.sb(top, [128, 4], F32, dma=True, name="pscale")
        blend = C.sb(top, [128, 2], F32, dma=True, name="blend")
        invc0 = C.sb(top, [128, 4, 128], F32, dma=True, name="invc0")
        invcc = C.sb(top, [128, 4, 128], F32, name="invcc")
        pen = C.sb(top, [128, 256], F32, dma=True, name="pen")
        ng = C.sb(top, [128, 8], F32, dma=True, name="ng")
        mg_ = C.sb(top, [128, 8], F32, dma=True, name="mg")
        halfpow = C.sb(top, [128, NBIS], F32, name="halfpow")

        C.load(ident.t[:], ident_d[:, :], ident)
        C.recycle = True

        def emit_pass(xsrc, w_in, norm_g, pool_w, pool_scale, mem_g, w_mkv, w_br, w_out,
                      ind_d, pen_d, invc_d, blend_d, runK, final, dst_rows):
            C.load(pscale.t[:], pool_scale[:, :], pscale)
            C.load(blend.t[:], blend_d[:, :], blend)
            C.load(invc0.t[:], invc_d[:, :, :], invc0)
            C.load(pen.t[:], pen_d[:, :], pen)
            C.load(ng.t[:], norm_g[:, :], ng)
            C.load(mg_.t[:], mem_g[:, :], mg_)
            for g, w in enumerate((2, 4, 8, 16)):
                C.memset("dve", invcc.t[:, g, :], 1.0 / w, [invcc])
            for k in range(NBIS):
                C.memset("dve", halfpow.t[:, k:k + 1], 2.0 ** (-k), [halfpow])

            with ExitStack() as st:
              if 'pro' in phases:
                indt = C.sb(st, [128, npairs, 128], BF16, dma=True, name="indt")
                rb = C.sb(st, [128, 32, 8], F32, dma=True, name="rb")
                eb = C.sb(st, [128, 32, 8], F32, name="eb")
                pwst = C.sb(st, [128, 4, 128], F32, dma=True, name="pwst")
                C.load(indt.t[:], ind_d[:, :, :], indt)
                C.load(rb.t[:].rearrange("p b h -> p (b h)"), relb[0:1, :].partition_broadcast(128), rb)
                C.load(pwst.t[:], pool_w.rearrange("g c d -> c g d"), pwst)
                C.cp("dve", poolW.t[:], pwst.t[:], [pwst], [poolW])
                C.tt("dve", eb.t[:], rb.t[:], rb.t[:, 15:16, :].broadcast_to([128, 32, 8]), ALU.subtract, [rb], [eb])
                C.act(eb.t[:], eb.t[:], AF.Exp, [eb], [eb])
                C.memset("dve", EM.t[:], 0.0, [EM])
                for pi, (n, b) in enumerate(_pairs()):
                    for h in range(8):
                        C.stt(EM.t[:, n, h, :], indt.t[:, pi, :], eb.t[:, b, h:h + 1], EM.t[:, n, h, :],
                              ALU.mult, ALU.add, [indt, eb, EM], [EM])

                wm = C.sb(st, [128, 8, 1024], BF16, name="wm")
                wst = [C.sb(st, [128, 1024], F32, dma=True, name="wst") for _ in range(2)]
                for kc in range(8):
                    b_ = wst[kc % 2]
                    C.load(b_.t[:], w_mkv[kc * 128:(kc + 1) * 128, :], b_)
                    C.ts("dve" if kc % 2 else "pool", wm.t[:, kc, :], b_.t[:], mg_.t[:, kc:kc + 1], None, ALU.mult, None,
                         [b_, mg_], [wm])
                mt_ = C.sb(st, [128, 2, 1024], F32, dma=True, name="memt")
                C.load(mt_.t[:], mem.rearrange("(a p) d -> p a d", p=128), mt_)
                sqj = C.sb(st, [128, 1024], BF16, name="sqj")
                ss = C.sb(st, [128, 2], F32, name="ss")
                memn = C.sb(st, [128, 2, 1024], BF16, name="memn")
                memnT = C.sb(st, [128, 8, 256], BF16, name="memnT")
                for a in range(2):
                    C.act(sqj.t[:], mt_.t[:, a, :], AF.Square, [mt_], [sqj, ss], accum_out=ss.t[:, a:a + 1])
                C.act(ss.t[:], ss.t[:], AF.Sqrt, [ss], [ss], scale=1.0 / D, bias=EPS)
                C.recip(ss.t[:], ss.t[:], [ss], [ss])
                for a in range(2):
                    C.act(memn.t[:, a, :], mt_.t[:, a, :], AF.Copy, [mt_, ss], [memn], scale=ss.t[:, a:a + 1])
                ptr = C.ps(st, [128, 1024], BF16, name="ptr")
                pmm = [C.ps(st, [128, 512], F32, name="pmm") for _ in range(2)]
                for kc in range(8):
                    for a in range(2):
                        C.tr(ptr.t[:, a * 128:(a + 1) * 128], memn.t[:, a, kc * 128:(kc + 1) * 128], ident.t[:],
                             [memn, ident], [ptr])
                    C.cp_rr(memnT.t[:, kc, :], ptr.t[:, 0:256], [ptr], [memnT])
                for h in range(4):
                    p_ = pmm[h % 2]
                    for kc in range(8):
                        C.mm(p_.t[:, 0:256], wm.t[:, kc, h * 128:(h + 1) * 128], memnT.t[:, kc, :], kc == 0, kc == 7,
                             [wm, memnT], [p_])
                    C.cp_rr(mkT.t[:, h, :], p_.t[:, 0:256], [p_], [mkT])
                C.memset("dve", mv.t[:], 1.0, [mv])
                for a in range(2):
                    p_ = pmm[a % 2]
                    for kc in range(8):
                        C.mm(p_.t[:], memnT.t[:, kc, a * 128:(a + 1) * 128], wm.t[:, kc, 512:1024], kc == 0, kc == 7,
                             [wm, memnT], [p_])
                    C.cp_rr(mv.t[:, a, :, 0:128], p_.t[:].rearrange("p (h d) -> p h d", h=4), [p_], [mv])
                S.barrier()
                S.emit()
                C.end_phase()

            with ExitStack() as st:
              if 'K' in phases and runK:
                NK = 1664
                wk = C.sb(st, [128, 8, NK], BF16, name="wk")
                wst = [C.sb(st, [128, NK], F32, dma=True, name="wstk") for _ in range(2)]
                for kc in range(8):
                    b_ = wst[kc % 2]
                    rows = slice(kc * 128, (kc + 1) * 128)
                    C.load(b_.t[:, 0:512], w_in[rows, C_AK:C_AK + 512], b_)
                    C.load(b_.t[:, 512:576], w_in[rows, C_IK:C_IK + 64], b_)
                    C.load(b_.t[:, 576:640], w_in[rows, C_IK:C_IK + 64], b_)
                    C.load(b_.t[:, 640:1152], w_in[rows, C_PU:C_PU + 512], b_)
                    C.load(b_.t[:, 1152:1664], w_in[rows, C_AV:C_AV + 512], b_)
                    C.ts("dve" if kc % 2 else "pool", wk.t[:, kc, :], b_.t[:], ng.t[:, kc:kc + 1], None, ALU.mult, None,
                         [b_, ng], [wk])
                zt = C.sb(st, [128, 4, 16], F32, dma=True, name="zt")
                C.memset("dve", zt.t[:], 0.0, [zt])
                C.store(u_scr[:, :, 0:16], zt.t[:], zt, [sres("u", -1)])
                xt = [C.sb(st, [128, 4, D], F32, dma=True, name="xt") for _ in range(2)]
                sqj = C.sb(st, [128, D], BF16, name="sqj")
                ss = [C.sb(st, [128, 4], F32, name="ss") for _ in range(2)]
                xn = [C.sb(st, [128, 4, D], BF16, name="xn") for _ in range(2)]
                xnT = [C.sb(st, [128, 8, 512], BF16, name="xnT") for _ in range(2)]
                kst = [C.sb(st, [128, 4, 512], BF16, dma=True, name="kst") for _ in range(2)]
                ikst = [C.sb(st, [128, 512], BF16, dma=True, name="ikst") for _ in range(2)]
                ust = [C.sb(st, [128, 4, 512], F32, dma=True, name="ust") for _ in range(2)]
                vst = [C.sb(st, [128, 4, 8, 65], BF16, dma=True, name="vst") for _ in range(2)]
                for b_ in vst:
                    C.memset("dve", b_.t[:], 1.0, [b_])
                ptr = [C.ps(st, [128, 1024], BF16, name="ptr") for _ in range(2)]
                pf = [C.ps(st, [128, 512], F32, name="pf") for _ in range(2)]
                pv = [C.ps(st, [128, 512], F32, name="pv") for _ in range(2)]
                for T in range(16):
                    x_ = xt[T % 2]
                    s_ = ss[T % 2]
                    n_ = xn[T % 2]
                    nT = xnT[T % 2]
                    C.load(x_.t[:], xsrc[T * 512:(T + 1) * 512, :].rearrange("(a p) d -> p a d", p=128), x_,
                           [sres("xfull", T)])
                    for a in range(4):
                        C.act(sqj.t[:], x_.t[:, a, :], AF.Square, [x_], [sqj, s_], accum_out=s_.t[:, a:a + 1])
                    C.act(s_.t[:], s_.t[:], AF.Sqrt, [s_], [s_], scale=1.0 / D, bias=EPS)
                    C.recip(s_.t[:], s_.t[:], [s_], [s_])
                    for a in range(4):
                        if a % 2 == 0:
                            C.act(n_.t[:, a, :], x_.t[:, a, :], AF.Copy, [x_, s_], [n_], scale=s_.t[:, a:a + 1])
                        else:
                            C.ts("dve", n_.t[:, a, :], x_.t[:, a, :], s_.t[:, a:a + 1], None, ALU.mult, None, [x_, s_], [n_])
                    for kc in range(8):
                        p_ = ptr[kc % 2]
                        for a in range(4):
                            C.tr(p_.t[:, a * 128:(a + 1) * 128], n_.t[:, a, kc * 128:(kc + 1) * 128], ident.t[:],
                                 [n_, ident], [p_])
                        C.cp_rr(nT.t[:, kc, :], p_.t[:, 0:512], [p_], [nT])
                    ks, iks, us, vs = kst[T % 2], ikst[T % 2], ust[T % 2], vst[T % 2]
                    for ct in range(9):
                        p_ = pf[ct % 2]
                        for kc in range(8):
                            C.mm(p_.t[:], wk.t[:, kc, ct * 128:(ct + 1) * 128], nT.t[:, kc, :], kc == 0, kc == 7,
                                 [wk, nT], [p_])
                        if ct < 4:
                            C.cp_rr(ks.t[:, ct, :], p_.t[:], [p_], [ks])
                        elif ct == 4:
                            C.cp_rr(iks.t[:], p_.t[:], [p_], [iks])
                        else:
                            C.cp_rr(us.t[:, ct - 5, :], p_.t[:], [p_], [us])
                    C.store(k_scr[:, :, T * 512:(T + 1) * 512], ks.t[:], ks, [sres("k", T)])
                    C.store(ik_scr[:, T * 512:(T + 1) * 512], iks.t[:], iks, [sres("ik", T)])
                    C.store(u_scr[:, :, 16 + T * 512:16 + (T + 1) * 512], us.t[:], us, [sres("u", T)])
                    for a in range(4):
                        p_ = pv[a % 2]
                        for kc in range(8):
                            C.mm(p_.t[:], nT.t[:, kc, a * 128:(a + 1) * 128], wk.t[:, kc, 1152:1664], kc == 0, kc == 7,
                                 [wk, nT], [p_])
                        C.cp_rr(vs.t[:, a, :, 0:64], p_.t[:].rearrange("p (h d) -> p h d", h=8), [p_], [vs])
                    C.store(v_scr[T * 512:(T + 1) * 512, :].rearrange("(a p) c -> p a c", p=128),
                            vs.t[:].rearrange("p a h c -> p a (h c)"), vs, [sres("v", T)])
                S.barrier()
                S.emit()
                C.end_phase()

            with ExitStack() as st:
              if 'P' in phases:
                NF = 2048
                NTM = 4104
                NP = NF + NTM
                wp = C.sb(st, [128, 8, NP], BF16, name="wp")
                tmst = [C.sb(st, [128, 2048], F32, dma=True, name="tmst") for _ in range(2)]
                cnt = 0
                for kc in range(8):
                    rows = slice(kc * 128, (kc + 1) * 128)
                    groups = [
                        (0, [(C_PZ, 512), (C_AQ, 512), (C_IQ, 512), (C_MQ, 512)]),
                        (2048, [(C_AZ, 512), (C_MZ, 512), (C_G, 1024)]),
                        (4096, [(C_G + 1024, 2048)]),
                        (6144, [(C_IW, 8)]),
                    ]
                    for (dst0, parts) in groups:
                        b_ = tmst[cnt % 2]
                        o = 0
                        for (c0, n) in parts:
                            C.load(b_.t[:, o:o + n], w_in[rows, c0:c0 + n], b_)
                            o += n
                        C.ts("dve" if cnt % 2 else "pool", wp.t[:, kc, dst0:dst0 + o], b_.t[:, 0:o], ng.t[:, kc:kc + 1],
                             None, ALU.mult, None, [b_, ng], [wp])
                        cnt += 1
                cand = [C.sb(st, [128, 4, D], F32, dma=True, name="cand") for _ in range(1)]
                xo = [C.sb(st, [128, 2, D], F32, dma=True, name="xo") for _ in range(1)]
                sqj = C.sb(st, [128, D], BF16, name="sqj")
                ss = [C.sb(st, [128, 2], F32, name="ss") for _ in range(2)]
                xn = C.sb(st, [128, 2, D], BF16, name="xn")
                xnT = [C.sb(st, [128, 8, 256], BF16, name="xnT") for _ in range(1)]
                szst = [C.sb(st, [128, 4, 256], F32, dma=True, name="szst") for _ in range(1)]
                fmst = [[C.sb(st, [128, 4, 256], BF16, dma=True, name="fmst") for _ in range(1)] for _ in range(3)]
                iwst = [C.sb(st, [128, 8], F32, dma=True, name="iwst") for _ in range(2)]
                ptr = [C.ps(st, [128, 1024], BF16, name="ptr") for _ in range(2)]
                pf = [C.ps(st, [128, 512], F32, name="pf") for _ in range(2)]
                pt = [C.ps(st, [128, 512], F32, name="pt") for _ in range(3)]
                for U in range(16):
                    c_ = cand[0]
                    x_ = xo[0]
                    s_ = ss[U % 2]
                    nT = xnT[0]
                    C.load(c_.t[:], xsrc[U * 512:(U + 1) * 512, :].rearrange("(a p) d -> p a d", p=128), c_,
                           [sres("xfull", U)])
                    C.ts("dve", x_.t[:], c_.t[:, 0:4:2, :], blend.t[:, 0:1], None, ALU.mult, None, [c_, blend], [x_])
                    C.stt(x_.t[:], c_.t[:, 1:4:2, :], blend.t[:, 1:2], x_.t[:], ALU.mult, ALU.add, [c_, blend, x_], [x_])
                    C.store(xo_scr[U * 256:(U + 1) * 256, :].rearrange("(a p) d -> p a d", p=128), x_.t[:], x_,
                            [sres("xo", U)])
                    for a in range(2):
                        C.act(sqj.t[:], x_.t[:, a, :], AF.Square, [x_], [sqj, s_], accum_out=s_.t[:, a:a + 1])
                    C.act(s_.t[:], s_.t[:], AF.Sqrt, [s_], [s_], scale=1.0 / D, bias=EPS)
                    C.recip(s_.t[:], s_.t[:], [s_], [s_])
                    C.act(xn.t[:, 0, :], x_.t[:, 0, :], AF.Copy, [x_, s_], [xn], scale=s_.t[:, 0:1])
                    C.ts("dve", xn.t[:, 1, :], x_.t[:, 1, :], s_.t[:, 1:2], None, ALU.mult, None, [x_, s_], [xn])
                    for kc in range(8):
                        p_ = ptr[kc % 2]
                        for a in range(2):
                            C.tr(p_.t[:, a * 128:(a + 1) * 128], xn.t[:, a, kc * 128:(kc + 1) * 128], ident.t[:],
                                 [xn, ident], [p_])
                        C.cp_rr(nT.t[:, kc, :], p_.t[:, 0:256], [p_], [nT])
                    szs = szst[0]
                    fms = [fmst[k][0] for k in range(3)]
                    for ct in range(16):
                        p_ = pf[ct % 2]
                        for kc in range(8):
                            C.mm(p_.t[:, 0:256], wp.t[:, kc, ct * 128:(ct + 1) * 128], nT.t[:, kc, :], kc == 0, kc == 7,
                                 [wp, nT], [p_])
                        if ct < 4:
                            C.act(szs.t[:, ct, :], p_.t[:, 0:256], AF.Silu, [p_], [szs])
                        else:
                            f_ = fms[ct // 4 - 1]
                            C.cp("dve", f_.t[:, ct % 4, :], p_.t[:, 0:256], [p_], [f_])
                    col = slice(U * 256, (U + 1) * 256)
                    C.store(sz_scr[:, :, col], szs.t[:], szs, [sres("sz", U)])
                    C.store(q_scr[:, :, col], fms[0].t[:], fms[0], [sres("q", U)])
                    C.store(iq_scr[:, :, col], fms[1].t[:], fms[1], [sres("iq", U)])
                    C.store(mq_scr[:, :, col], fms[2].t[:], fms[2], [sres("mq", U)])
                    for a in range(2):
                        iws = iwst[a]
                        rows = slice(U * 256 + a * 128, U * 256 + (a + 1) * 128)
                        for grp in range(8):
                            tm = tmst[grp // 4]
                            p_ = pt[grp % 3]
                            for kc in range(8):
                                C.mm(p_.t[:], nT.t[:, kc, a * 128:(a + 1) * 128],
                                     wp.t[:, kc, NF + grp * 512:NF + (grp + 1) * 512], kc == 0, kc == 7, [wp, nT], [p_])
                            C.act(tm.t[:, (grp % 4) * 512:(grp % 4 + 1) * 512], p_.t[:], AF.Silu if grp < 2 else AF.Sigmoid,
                                  [p_], [tm])
                            if grp == 3:
                                C.store(saz_scr[rows, :], tm.t[:, 0:512], tm, [sres("saz", 2 * U + a)])
                                C.store(smz_scr[rows, :], tm.t[:, 512:1024], tm, [sres("smz", 2 * U + a)])
                                C.store(sg_scr[rows, 0:1024], tm.t[:, 1024:2048], tm, [sres("sg", 2 * U + a)])
                            if grp == 7:
                                C.store(sg_scr[rows, 1024:3072], tm.t[:, 0:2048], tm, [sres("sg", 2 * U + a)])
                        p_ = pt[2]
                        for kc in range(8):
                            C.mm(p_.t[:, 0:8], nT.t[:, kc, a * 128:(a + 1) * 128], wp.t[:, kc, NF + 4096:NF + 4104],
                                 kc == 0, kc == 7, [wp, nT], [p_])
                        C.ts("dve", iws.t[:], p_.t[:, 0:8], IDX_W_SCALE, None, ALU.mult, None, [p_], [iws])
                        C.store(iw_scr[rows, :], iws.t[:], iws, [sres("iw", 2 * U + a)])
                S.barrier()
                S.emit()
                C.end_phase()

            with ExitStack() as st:
              if 'A1' in phases:
                scores = C.sb(st, [128, NT], F32, name="scores")
                mask01 = [C.sb(st, [128, NT], BF16, name="mask01") for _ in range(2)]
                mT = [C.sb(st, [128, 4, 128], BF16, name="mT") for _ in range(2)]
                kTc = [C.sb(st, [128, 4, 512], BF16, dma=True, name="kTc") for _ in range(2)]
                Vc = [C.sb(st, [128, 4, 520], BF16, dma=True, name="Vc") for _ in range(2)]
                ikt = [C.sb(st, [128, 512], BF16, dma=True, name="ikt") for _ in range(2)]
                Rt = [[C.sb(st, [128, 512], BF16, name="Rt") for _ in range(8)] for _ in range(2)]
                Et = [C.sb(st, [128, 4, 128], BF16, name="Et") for _ in range(2)]
                Et2 = [C.sb(st, [128, 4, 128], BF16, name="Et2") for _ in range(2)]
                Pt = [C.sb(st, [128, 4, 128], BF16, name="Pt") for _ in range(2)]
                qT = [C.sb(st, [128, 4, 128], BF16, dma=True, name="qT") for _ in range(2)]
                iqT = [C.sb(st, [128, 4, 128], BF16, dma=True, name="iqT") for _ in range(2)]
                mqT = [C.sb(st, [128, 4, 128], BF16, dma=True, name="mqT") for _ in range(2)]
                iw = [C.sb(st, [128, 8], F32, dma=True, name="iw") for _ in range(2)]
                qz = [C.sb(st, [128, 8, 128], BF16, name="qz") for _ in range(2)]
                iqz = [C.sb(st, [128, 8, 128], BF16, name="iqz") for _ in range(2)]
                for b_ in qz + iqz:
                    C.memset("dve", b_.t[:], 0.0, [b_])
                Dg = [C.sb(st, [128, 8, 128], BF16, name="Dg") for _ in range(2)]
                szT = [C.sb(st, [128, 4, 128], F32, dma=True, name="szT") for _ in range(2)]
                saz = [C.sb(st, [128, 512], F32, dma=True, name="saz") for _ in range(2)]
                smz = [C.sb(st, [128, 512], F32, dma=True, name="smz") for _ in range(2)]
                ua = [C.sb(st, [128, 4, 144], F32, dma=True, name="ua") for _ in range(2)]
                ub = [C.sb(st, [128, 4, 144], F32, dma=True, name="ub") for _ in range(2)]
                uw = C.sb(st, [128, 4, 144], F32, name="uw")
                s1 = C.sb(st, [128, 4, 144], F32, name="s1")
                s2 = C.sb(st, [128, 4, 144], F32, name="s2")
                s3 = C.sb(st, [128, 4, 144], F32, name="s3")
                Sall = C.sb(st, [128, 4, 128], F32, name="Sall")
                plb = C.sb(st, [128, 4, 128], BF16, name="plb")
                ypf = C.sb(st, [128, 4, 128], F32, name="ypf")
                amax = C.sb(st, [128, 20], F32, name="amax")
                A_ = C.sb(st, [128, 1], F32, name="A")
                steps = C.sb(st, [128, NBIS], F32, name="steps")
                lo = C.sb(st, [128, 1], F32, name="lo")
                cc = C.sb(st, [128, 1], F32, name="cc")
                cnt_ = C.sb(st, [128, 1], F32, name="cnt")
                cntb = C.sb(st, [128, 8], F32, name="cntb")
                dd = C.sb(st, [128, 1], F32, name="dd")
                rs = C.sb(st, [128, 8], F32, name="rs")
                accs = [C.sb(st, [128, 4, 65], F32, name="accs") for _ in range(2)]
                yaf = C.sb(st, [128, 8, 64], F32, name="yaf")
                yab = C.sb(st, [128, 512], BF16, name="yab")
                rsm = C.sb(st, [128, 4], F32, name="rsm")
                ymf = C.sb(st, [128, 4, 128], F32, name="ymf")
                ymb = C.sb(st, [128, 512], BF16, name="ymb")
                Pm = C.sb(st, [128, 8, 128], BF16, name="Pm")
                yst = [C.sb(st, [128, 3, 4, 128], BF16, dma=True, name="yst") for _ in range(2)]
                psc = [C.ps(st, [128, 512], F32, name="psc") for _ in range(2)]
                pidx = C.ps(st, [128, 512], F32, name="pidx")
                pmT = C.ps(st, [128, 1024], BF16, name="pmT")
                pl = [C.ps(st, [128, 4, 128], F32, name="pl") for _ in range(2)]
                pacc = [C.ps(st, [128, 512], F32, name="pacc") for _ in range(2)]

                NS = int(os.environ.get('A1_SLOTS', NSLOT))
                MASKENG = os.environ.get('MASKENG', 'dve')

                def idx_tiles(i):
                    L = (2 * i + 2) * 128
                    tiles = []
                    k0 = 0
                    if (L // 256) % 2 == 1:
                        tiles.append((0, 256))
                        k0 = 256
                    while k0 < L:
                        tiles.append((k0, 512))
                        k0 += 512
                    return tiles

                def prep(i):
                    par = i % 2
                    col = slice(i * 128, (i + 1) * 128)
                    rows = slice(i * 128, (i + 1) * 128)
                    U = i // 2
                    q_, iq_, mq_, iw_, Dg_ = qT[par], iqT[par], mqT[par], iw[par], Dg[par]
                    sz_, saz_, smz_, ua_, ub_ = szT[par], saz[par], smz[par], ua[par], ub[par]
                    C.load(q_.t[:], q_scr[:, :, col], q_, [sres("q", U)])
                    C.load(iq_.t[:], iq_scr[:, :, col], iq_, [sres("iq", U)])
                    C.load(mq_.t[:], mq_scr[:, :, col], mq_, [sres("mq", U)])
                    C.load(iw_.t[:], iw_scr[rows, :], iw_, [sres("iw", i)])
                    C.load(sz_.t[:], sz_scr[:, :, col], sz_, [sres("sz", U)])
                    C.load(saz_.t[:], saz_scr[rows, :], saz_, [sres("saz", i)])
                    C.load(smz_.t[:], smz_scr[rows, :], smz_, [sres("smz", i)])
                    C.load(ua_.t[:], u_scr[:, :, 256 * i:256 * i + 144], ua_, [sres("u", -1)])
                    C.load(ub_.t[:], u_scr[:, :, 256 * i + 128:256 * i + 272], ub_, [sres("u", -1)])
                    qz_, iqz_ = qz[par], iqz[par]
                    for r in range(2):
                        ps_ = slice(r * 64, (r + 1) * 64)
                        C.cp("dve", qz_.t[ps_, r:8:2, :], q_.t[ps_, :, :], [q_], [qz_])
                        C.cp("dve", iqz_.t[ps_, r:8:2, :], iq_.t[ps_, :, :], [iq_], [iqz_])
                    C.tt("dve", Dg_.t[:], ident.t[:].unsqueeze(1).broadcast_to([128, 8, 128]),
                         iw_.t[:].unsqueeze(2).broadcast_to([128, 8, 128]), ALU.mult, [ident, iw_], [Dg_])

                def gen_idx(i):
                    par = i % 2
                    Dg_, iqz_ = Dg[par], iqz[par]
                    tiles = idx_tiles(i)
                    for ti, (k0, kw) in enumerate(tiles):
                        ik_ = ikt[ti % 2]
                        R_ = Rt[ti % 2]
                        C.load(ik_.t[:, 0:kw], ik_scr[:, k0:k0 + kw], ik_, [sres("ik", k0 // 512)])

                        def sc(h):
                            p_ = psc[h % 2]
                            C.mm(p_.t[:, 0:kw], iqz_.t[:, h, :], ik_.t[:, 0:kw], True, True, [iqz_, ik_], [p_])

                        def relu(h):
                            p_ = psc[h % 2]
                            C.act(R_[h].t[:, 0:kw], p_.t[:, 0:kw], AF.Relu, [p_], [R_[h]])

                        def red(h):
                            C.mm(pidx.t[:, 0:kw], Dg_.t[:, h, :], R_[h].t[:, 0:kw], h == 0, h == 7, [Dg_, R_[h]], [pidx])
                        sc(0)
                        sc(1)
                        for h in range(8):
                            relu(h)
                            if h + 2 < 8:
                                sc(h + 2)
                            red(h)
                            yield
                        C.reduce(amax.t[:, ti:ti + 1], pidx.t[:, 0:kw], ALU.max, [pidx], [amax], absv=True)
                        if ti == len(tiles) - 1:
                            if kw > 256:
                                C.cp("act", scores.t[:, k0:k0 + kw - 256], pidx.t[:, 0:kw - 256], [pidx], [scores])
                            C.tt("dve", scores.t[:, k0 + kw - 256:k0 + kw], pidx.t[:, kw - 256:kw], pen.t[:], ALU.add,
                                 [pidx, pen], [scores])
                        else:
                            C.cp("act", scores.t[:, k0:k0 + kw], pidx.t[:, 0:kw], [pidx], [scores])
                        yield

                BW = 1024

                def gen_bis(i):
                    L = (2 * i + 2) * 128
                    nblk = (L + BW - 1) // BW
                    m01 = mask01[i % 2]
                    C.reduce(A_.t[:], amax.t[:, 0:len(idx_tiles(i))], ALU.max, [amax], [A_])
                    C.ts("dve", A_.t[:], A_.t[:], 1.0001, 1e-20, ALU.mult, ALU.add, [A_], [A_])
                    C.ts("dve", steps.t[:], halfpow.t[:], A_.t[:, 0:1], None, ALU.mult, None, [halfpow, A_], [steps])
                    C.ts("dve", lo.t[:], A_.t[:], -1.0, None, ALU.mult, None, [A_], [lo])
                    yield
                    for k in range(NBIS):
                        C.tt("dve", cc.t[:], lo.t[:], steps.t[:, k:k + 1], ALU.add, [lo, steps], [cc])
                        for bk in range(nblk):
                            c0, c1 = bk * BW, min(L, (bk + 1) * BW)
                            C.ts("dve", m01.t[:, c0:c1], scores.t[:, c0:c1], cc.t[:, 0:1], 0.0, ALU.is_ge, ALU.add,
                                 [scores, cc], [m01, cntb], accum_out=cntb.t[:, bk:bk + 1])
                            yield
                        C.reduce(cnt_.t[:], cntb.t[:, 0:nblk], ALU.add, [cntb], [cnt_])
                        C.stt(dd.t[:], cnt_.t[:], TOPK - 0.5, steps.t[:, k:k + 1], ALU.is_ge, ALU.mult,
                              [cnt_, steps], [dd])
                        C.tt("dve", lo.t[:], lo.t[:], dd.t[:], ALU.add, [lo, dd], [lo])
                        yield
                    for bk in range(nblk):
                        c0, c1 = bk * BW, min(L, (bk + 1) * BW)
                        C.ts("dve", m01.t[:, c0:c1], scores.t[:, c0:c1], lo.t[:, 0:1], None, ALU.is_ge, None,
                             [scores, lo], [m01])
                        yield

                def bis_units(i):
                    nblk = ((2 * i + 2) * 128 + BW - 1) // BW
                    return 1 + NBIS * (nblk + 1) + nblk

                def gen_att(i):
                    par = i % 2
                    nkt = 2 * i + 2
                    col = slice(i * 128, (i + 1) * 128)
                    mq_, qz_ = mqT[par], qz[par]
                    sz_, saz_, smz_, ua_, ub_ = szT[par], saz[par], smz[par], ua[par], ub[par]
                    m01 = mask01[par]
                    steps_ = [(kt, hg) for kt in range(nkt) for hg in range(2)]

                    def chunk_setup(c):
                        kt0 = c * 4
                        nk = min(4, nkt - kt0)
                        kc_, vc_, mT_ = kTc[c % 2], Vc[c % 2], mT[c % 2]
                        C.load(kc_.t[:, :, 0:nk * 128], k_scr[:, :, kt0 * 128:(kt0 + nk) * 128], kc_,
                               [sres("k", (kt0 * 128) // 512)])
                        C.load(vc_.t[:, 0:nk, :], v_scr[kt0 * 128:(kt0 + nk) * 128, :].rearrange("(a p) c -> p a c", p=128),
                               vc_, [sres("v", (kt0 * 128) // 512)])
                        for jj in range(nk):
                            kt = kt0 + jj
                            C.tr(pmT.t[:, jj * 128:(jj + 1) * 128], m01.t[:, kt * 128:(kt + 1) * 128], ident.t[:],
                                 [m01, ident], [pmT])
                        C.cp("act", mT_.t[:, 0:nk, :], pmT.t[:, 0:nk * 128].rearrange("p (a t) -> p a t", a=nk),
                             [pmT], [mT_])

                    def stA(sidx):
                        kt, hg = steps_[sidx]
                        c, jj = kt // 4, kt % 4
                        if jj == 0 and hg == 0:
                            chunk_setup(c)
                        kc_ = kTc[c % 2]
                        p_ = pl[hg]
                        for hh in range(4):
                            h = hg * 4 + hh
                            C.mm(p_.t[:, hh, :], kc_.t[:, h // 2, jj * 128:(jj + 1) * 128],
                                 qz_.t[:, h, :], True, True, [kc_, qz_], [p_])

                    def stB(sidx):
                        kt, hg = steps_[sidx]
                        c, jj = kt // 4, kt % 4
                        n = nkt - 1 - kt
                        mT_ = mT[c % 2]
                        p_, e_, P_ = pl[hg], Et[hg], Pt[hg]
                        C.act(e_.t[:], p_.t[:], AF.Exp, [p_], [e_], scale=ATTN_SCALE)
                        mb = mT_.t[:, jj:jj + 1, :].broadcast_to([128, 4, 128])
                        if n < NNEAR:
                            e2 = Et2[hg]
                            C.tt("dve", e2.t[:], e_.t[:], EM.t[:, n, hg * 4:(hg + 1) * 4, :], ALU.mult,
                                 [e_, EM], [e2])
                            C.tt(MASKENG, P_.t[:], e2.t[:], mb, ALU.mult, [e2, mT_], [P_])
                        else:
                            C.tt(MASKENG, P_.t[:], e_.t[:], mb, ALU.mult, [e_, mT_], [P_])

                    def stC(sidx):
                        kt, hg = steps_[sidx]
                        c, jj = kt // 4, kt % 4
                        vc_ = Vc[c % 2]
                        P_, a_ = Pt[hg], pacc[hg]
                        for hh in range(4):
                            h = hg * 4 + hh
                            C.mm(a_.t[:, hh * 65:(hh + 1) * 65], P_.t[:, hh, :], vc_.t[:, jj, h * 65:(h + 1) * 65],
                                 kt == 0 and hh == 0, kt == nkt - 1, [P_, vc_], [a_], nocheck=True)

                    stA(0)
                    stA(1)
                    for sidx in range(len(steps_)):
                        stB(sidx)
                        if sidx + 2 < len(steps_):
                            stA(sidx + 2)
                        stC(sidx)
                        yield
                    ys = yst[par]
                    for hg in range(2):
                        C.cp("dve", accs[hg].t[:], pacc[hg].t[:, 0:260].rearrange("p (h c) -> p h c", h=4),
                             [pacc[hg]], [accs[hg]])
                        av = accs[hg].t[:]
                        C.recip(rs.t[:, hg * 4:(hg + 1) * 4], av[:, :, 64], [accs[hg]], [rs])
                        C.tt("dve", yaf.t[:, hg * 4:(hg + 1) * 4, :], av[:, :, 0:64],
                             rs.t[:, hg * 4:(hg + 1) * 4].unsqueeze(2).broadcast_to([128, 4, 64]), ALU.mult,
                             [accs[hg], rs], [yaf])
                    C.tt("dve", yab.t[:], yaf.t[:].rearrange("p h d -> p (h d)"), saz_.t[:], ALU.mult, [yaf, saz_], [yab])
                    for kc in range(4):
                        C.tr(pmT.t[:, kc * 128:(kc + 1) * 128], yab.t[:, kc * 128:(kc + 1) * 128], ident.t[:],
                             [yab, ident], [pmT])
                    C.cp("act", ys.t[:, 1, :, :], pmT.t[:, 0:512].rearrange("p (a t) -> p a t", a=4), [pmT], [ys])
                    yield

                    for mt in range(2):
                        p_ = pl[mt]
                        for hm in range(4):
                            C.mm(p_.t[:, hm, :], mkT.t[:, hm, mt * 128:(mt + 1) * 128], mq_.t[:, hm, :], True, True,
                                 [mkT, mq_], [p_])
                        C.act(Pm.t[:, mt * 4:(mt + 1) * 4, :], p_.t[:], AF.Exp, [p_], [Pm], scale=MEM_SCALE)
                    for hm in range(4):
                        a_ = pacc[hm // 2]
                        o = (hm % 2) * 129
                        for mt in range(2):
                            C.mm(a_.t[:, o:o + 129], Pm.t[:, mt * 4 + hm, :], mv.t[:, mt, hm, :], mt == 0, mt == 1,
                                 [Pm, mv], [a_])
                    for hp in range(2):
                        av = pacc[hp].t[:, 0:258].rearrange("p (h c) -> p h c", h=2)
                        C.recip(rsm.t[:, hp * 2:(hp + 1) * 2], av[:, :, 128], [pacc[hp]], [rsm])
                        C.tt("dve", ymf.t[:, hp * 2:(hp + 1) * 2, :], av[:, :, 0:128],
                             rsm.t[:, hp * 2:(hp + 1) * 2].unsqueeze(2).broadcast_to([128, 2, 128]), ALU.mult,
                             [pacc[hp], rsm], [ymf])
                    C.tt("dve", ymb.t[:], ymf.t[:].rearrange("p h d -> p (h d)"), smz_.t[:], ALU.mult, [ymf, smz_], [ymb])
                    for kc in range(4):
                        C.tr(pmT.t[:, kc * 128:(kc + 1) * 128], ymb.t[:, kc * 128:(kc + 1) * 128], ident.t[:],
                             [ymb, ident], [pmT])
                    C.cp("act", ys.t[:, 2, :, :], pmT.t[:, 0:512].rearrange("p (a t) -> p a t", a=4), [pmT], [ys])
                    yield

                    C.ts("dve", uw.t[:], ua_.t[:], blend.t[:, 0:1], None, ALU.mult, None, [ua_, blend], [uw])
                    C.stt(uw.t[:], ub_.t[:], blend.t[:, 1:2], uw.t[:], ALU.mult, ALU.add, [ub_, blend, uw], [uw])
                    C.tt("dve", s1.t[:, :, 1:144], uw.t[:, :, 1:144], uw.t[:, :, 0:143], ALU.add, [uw], [s1])
                    C.tt("dve", s2.t[:, 1:4, 3:144], s1.t[:, 1:4, 3:144], s1.t[:, 1:4, 1:142], ALU.add, [s1], [s2])
                    C.tt("dve", s3.t[:, 2:4, 7:144], s2.t[:, 2:4, 7:144], s2.t[:, 2:4, 3:140], ALU.add, [s2], [s3])
                    C.cp("dve", Sall.t[:, 0, :], s1.t[:, 0, 16:144], [s1], [Sall])
                    C.cp("dve", Sall.t[:, 1, :], s2.t[:, 1, 16:144], [s2], [Sall])
                    C.cp("dve", Sall.t[:, 2, :], s3.t[:, 2, 16:144], [s3], [Sall])
                    C.tt("dve", Sall.t[:, 3, :], s3.t[:, 3, 16:144], s3.t[:, 3, 8:136], ALU.add, [s3], [Sall])
                    ic = invc0 if i == 0 else invcc
                    C.tt("dve", Sall.t[:], Sall.t[:], ic.t[:], ALU.mult, [Sall, ic], [Sall])
                    C.tt("dve", plb.t[:], Sall.t[:], uw.t[:, :, 16:144], ALU.subtract, [Sall, uw], [plb])
                    pp = pl[1]
                    for g in range(4):
                        C.mm(pp.t[:, g, :], poolW.t[:, g, :], plb.t[:, g, :], True, True, [poolW, plb], [pp])
                    C.tt("dve", ypf.t[:], pp.t[:], pscale.t[:].unsqueeze(2).broadcast_to([128, 4, 128]), ALU.mult,
                         [pp, pscale], [ypf])
                    C.tt("dve", ys.t[:, 0, :, :], ypf.t[:], sz_.t[:], ALU.mult, [ypf, sz_], [ys])
                    C.store(y_scr[:, :, :, col], ys.t[:], ys, [sres("y", i)])
                    yield

                SENT = object()
                prep(0)
                for _ in gen_idx(0):
                    pass
                for _ in gen_bis(0):
                    pass
                for i in range(NS):
                    if i + 1 < NS:
                        prep(i + 1)
                        side = itertools.chain(gen_idx(i + 1), gen_bis(i + 1))
                        n_side = 9 * len(idx_tiles(i + 1)) + bis_units(i + 1)
                    else:
                        side = iter(())
                        n_side = 0
                    n_main = 2 * (2 * i + 2) + 3
                    done = 0
                    for m, _ in enumerate(gen_att(i)):
                        target = ((m + 1) * n_side + n_main - 1) // n_main
                        while done < target:
                            if next(side, SENT) is SENT:
                                done = n_side
                                break
                            done += 1
                    for _ in side:
                        pass
                S.barrier()
                S.emit()
                C.end_phase()

            with ExitStack() as st:
              if 'A2' in phases:
                wb = C.sb(st, [128, 3, 4, D], BF16, name="wb")
                wo = C.sb(st, [128, 8, D], BF16, name="wo")
                wst = [C.sb(st, [128, D], F32, dma=True, name="wsta") for _ in range(2)]
                cnt = 0
                for br in range(3):
                    for kc in range(4):
                        b_ = wst[cnt % 2]
                        C.load(b_.t[:], w_br[br, kc * 128:(kc + 1) * 128, :], b_)
                        C.cp("dve" if cnt % 2 else "pool", wb.t[:, br, kc, :], b_.t[:], [b_], [wb])
                        cnt += 1
                for kc in range(8):
                    b_ = wst[cnt % 2]
                    C.load(b_.t[:], w_out[kc * 128:(kc + 1) * 128, :], b_)
                    C.cp("dve" if cnt % 2 else "pool", wo.t[:, kc, :], b_.t[:], [b_], [wo])
                    cnt += 1
                fg = C.sb(st, [128, D], F32, dma=True, name="fg")
                C.load(fg.t[:], fin_g[0:1, :].partition_broadcast(128), fg)
                yT = [C.sb(st, [128, 3, 4, 128], BF16, dma=True, name="yT") for _ in range(2)]
                sg = [C.sb(st, [128, 3072], F32, dma=True, name="sg") for _ in range(2)]
                xo = [C.sb(st, [128, D], F32, dma=True, name="xo2") for _ in range(2)]
                m1 = C.sb(st, [128, 512], F32, name="m1")
                m2 = C.sb(st, [128, 512], F32, name="m2")
                mgb = C.sb(st, [128, D], BF16, name="mgb")
                mgT = C.sb(st, [128, 8, 128], BF16, name="mgT")
                xnew = [C.sb(st, [128, D], F32, dma=True, name="xnew") for _ in range(2)]
                sqj = C.sb(st, [128, D], BF16, name="sqj")
                ssf = C.sb(st, [128, 1], F32, name="ssf")
                pb = [C.ps(st, [128, 512], F32, name="pb") for _ in range(3)]
                ptr = C.ps(st, [128, 1024], BF16, name="ptr")
                po = [C.ps(st, [128, 512], F32, name="po") for _ in range(2)]
                for i in range(NSLOT):
                    par = i % 2
                    col = slice(i * 128, (i + 1) * 128)
                    rows = slice(i * 128, (i + 1) * 128)
                    y_, g_, x_, xn_ = yT[par], sg[par], xo[par], xnew[par]
                    C.load(y_.t[:], y_scr[:, :, :, col], y_, [sres("y", i)])
                    C.load(g_.t[:], sg_scr[rows, :], g_, [sres("sg", i)])
                    C.load(x_.t[:], xo_scr[rows, :], x_, [sres("xo", i // 2)])
                    for half in range(2):
                        hs = slice(half * 512, (half + 1) * 512)
                        for br in range(3):
                            for kc in range(4):
                                C.mm(pb[br].t[:], y_.t[:, br, kc, :], wb.t[:, br, kc, hs], kc == 0, kc == 3,
                                     [y_, wb], [pb[br]])
                        C.tt("dve", m1.t[:], pb[0].t[:], g_.t[:, half * 512:half * 512 + 512], ALU.mult, [pb[0], g_], [m1])
                        C.tt("dve", m2.t[:], pb[1].t[:], g_.t[:, 1024 + half * 512:1024 + half * 512 + 512], ALU.mult,
                             [pb[1], g_], [m2])
                        C.tt("pool", m1.t[:], m1.t[:], m2.t[:], ALU.add, [m1, m2], [m1])
                        C.tt("dve", m2.t[:], pb[2].t[:], g_.t[:, 2048 + half * 512:2048 + half * 512 + 512], ALU.mult,
                             [pb[2], g_], [m2])
                        C.tt("pool", mgb.t[:, hs], m1.t[:], m2.t[:], ALU.add, [m1, m2], [mgb])
                    for kc in range(8):
                        C.tr(ptr.t[:, kc * 128:(kc + 1) * 128], mgb.t[:, kc * 128:(kc + 1) * 128], ident.t[:],
                             [mgb, ident], [ptr])
                    C.cp("act", mgT.t[:], ptr.t[:].rearrange("p (a t) -> p a t", a=8), [ptr], [mgT])
                    for half in range(2):
                        hs = slice(half * 512, (half + 1) * 512)
                        for kc in range(8):
                            C.mm(po[half].t[:], mgT.t[:, kc, :], wo.t[:, kc, hs], kc == 0, kc == 7, [mgT, wo], [po[half]])
                        C.tt("dve", xn_.t[:, hs], po[half].t[:], x_.t[:, hs], ALU.add, [po[half], x_], [xn_])
                    if final:
                        C.act(sqj.t[:], xn_.t[:], AF.Square, [xn_], [sqj, ssf], accum_out=ssf.t[:])
                        C.act(ssf.t[:], ssf.t[:], AF.Sqrt, [ssf], [ssf], scale=1.0 / D, bias=EPS)
                        C.recip(ssf.t[:], ssf.t[:], [ssf], [ssf])
                        C.stt(xn_.t[:], xn_.t[:], ssf.t[:, 0:1], fg.t[:], ALU.mult, ALU.mult, [xn_, ssf, fg], [xn_])
                    C.store(dst_rows(i), xn_.t[:], xn_, [sres("out", i)])
                S.barrier()
                S.emit()
                C.end_phase()

        emit_pass(xfull, runK=True, final=False, dst_rows=lambda i: x1full[(2 * i) * 128:(2 * i + 1) * 128, :],
                  **LW[0], **TAB["0"])
        emit_pass(xfull, runK=False, final=False, dst_rows=lambda i: x1full[(2 * i + 1) * 128:(2 * i + 2) * 128, :],
                  **LW[0], **TAB["1"])
        emit_pass(x1full, runK=True, final=True, dst_rows=lambda i: out_d[i * 128:(i + 1) * 128, :],
                  **LW[1], **TAB["o"])
        S.barrier()
        S.emit()
    return nc


_PROG = {}


def _get_prog():
    if "p" not in _PROG:
        _PROG["p"] = build_program()
    return _PROG["p"]


def _maps(inp):
    consts = [_tables(0), _tables(1)]
    maps = []
    for c in range(8):
        b, j = c // 2, c % 2
        m = {
            "xfull": np.ascontiguousarray(inp["x"][b]),
            "mem": np.ascontiguousarray(inp["mem"][b]),
            "relb": np.ascontiguousarray(inp["rel_bias"].reshape(1, 256)),
            "fin_g": np.ascontiguousarray(inp["final_g"].reshape(1, D)),
            "ident": np.eye(128).astype(ml_dtypes.bfloat16),
        }
        for l in range(2):
            m["w_in%d" % l] = np.ascontiguousarray(inp["w_in"][l])
            m["norm_g%d" % l] = np.ascontiguousarray(inp["norm_g"][l].reshape(8, 128).T)
            m["pool_w%d" % l] = np.ascontiguousarray(inp["pool_w"][l])
            m["pool_scale%d" % l] = np.ascontiguousarray(inp["pool_scale"][l].reshape(4, 128).T)
            m["mem_g%d" % l] = np.ascontiguousarray(inp["mem_norm_g"][l].reshape(8, 128).T)
            m["w_mkv%d" % l] = np.ascontiguousarray(inp["w_mem_kv"][l])
            m["w_br%d" % l] = np.ascontiguousarray(inp["w_branch"][l])
            m["w_out%d" % l] = np.ascontiguousarray(inp["w_out"][l])
        for tb, p in (("0", 0), ("1", 1), ("o", j)):
            ind, pen, invc = consts[p]
            blend = np.zeros((128, 2), np.float32)
            blend[:, p] = 1.0
            m["ind" + tb], m["pen" + tb], m["invc" + tb], m["blend" + tb] = ind, pen, invc, blend
        maps.append(m)
    return maps


def _assemble(results):
    full = np.empty((4, NB, 128, D), np.float32)
    for b in range(4):
        for j in range(2):
            full[b, j::2] = results[2 * b + j]["out"].reshape(NSLOT, 128, D)
    return full.reshape(4, NT, D)


def kernel(x, mem, norm_g, w_in, pool_w, pool_scale, mem_norm_g, w_mem_kv, w_branch, w_out, rel_bias, final_g):
    inp = {k: np.asarray(v, dtype=np.float32) for k, v in dict(
        x=x, mem=mem, norm_g=norm_g, w_in=w_in, pool_w=pool_w, pool_scale=pool_scale, mem_norm_g=mem_norm_g,
        w_mem_kv=w_mem_kv, w_branch=w_branch, w_out=w_out, rel_bias=rel_bias, final_g=final_g).items()}
    nc = _get_prog()
    res = run_bass_kernel_spmd(nc, _maps(inp), core_ids=list(range(8)))
    return _assemble(res.results)
```

```python
import itertools
import math
import os
import numpy as np
import ml_dtypes
from contextlib import ExitStack
import concourse.bass as bass
import concourse.mybir as mybir
from concourse.bass_utils import run_bass_kernel_spmd

F32 = mybir.dt.float32
BF16 = mybir.dt.bfloat16
AF = mybir.ActivationFunctionType
ALU = mybir.AluOpType
AX = mybir.AxisListType

D = 1024
NT = 8192
NB = 64
NSLOT = 32
NOWN = 4096
INC = 7752
EPS = 1e-6
NEG = -1e30
ATTN_SCALE = 64 ** -0.5
MEM_SCALE = 128 ** -0.5
IDX_W_SCALE = (64 ** -0.5) * (8 ** -0.5)
NBIS = 12
TOPK = 256
C_PU, C_PZ, C_AQ, C_AK, C_AV, C_AZ, C_IQ, C_IK, C_IW, C_MQ, C_MZ, C_G = (
    0, 512, 1024, 1536, 2048, 2560, 3072, 3584, 3648, 3656, 4168, 4680)
NNEAR = 7


class Res:
    __slots__ = ("name", "w", "r", "excl")

    def __init__(self, name="", excl=False):
        self.name = name
        self.w = None
        self.r = {}
        self.excl = excl


class DmaSem:
    __slots__ = ("key", "sem", "count")

    def __init__(self, key, sem):
        self.key = key
        self.sem = sem
        self.count = 0


class Sched:
    ENGS = ("pe", "act", "dve", "pool", "sp")

    def __init__(self, nc, stack):
        self.nc = nc
        self.stack = stack
        self.sems = {}
        self.tick = {}
        self.waited = {}
        self.ops = {}
        self.dmasems = []
        self.free = []
        self.epoch = -1
        self.nins = 0
        for e in self.ENGS:
            self.ops[e] = []
        self._new_epoch()

    def _ekey(self, e):
        return "E_%s_%d" % (e, self.epoch)

    def _new_epoch(self):
        self.epoch += 1
        for e in self.ENGS:
            phys = "P_%s_%d" % (e, self.epoch % 3)
            if phys not in self.sems:
                self.sems[phys] = self.stack.enter_context(self.nc.semaphore("sem_%s_%d" % (e, self.epoch % 3)))
            self.sems[self._ekey(e)] = self.sems[phys]
            self.tick[e] = 0
            self.waited[e] = {}
            if self.epoch >= 2:
                nxt = "P_%s_%d" % (e, (self.epoch + 1) % 3)
                self.ops[e].append(((), ("clear", nxt), None))

    def dma_sem(self):
        if self.free:
            return self.free.pop()
        key = "D_%d" % len(self.dmasems)
        sem = self.stack.enter_context(self.nc.semaphore("dsem_%d" % len(self.dmasems)))
        self.sems[key] = sem
        d = DmaSem(key, sem)
        self.dmasems.append(d)
        return d

    def _need(self, eng, waits, ev, same_ok):
        if ev is None:
            return
        key, val, src = ev
        if key.startswith("E_") and not key.endswith("_%d" % self.epoch):
            return
        if src == eng and not same_ok:
            return
        if self.waited[eng].get(key, 0) >= val:
            return
        if waits.get(key, 0) < val:
            waits[key] = val

    def op(self, eng, fn, reads=(), writes=(), dma=None):
        waits = {}
        raw_same = (eng != "pe")
        for R in reads:
            self._need(eng, waits, R.w, raw_same)
            if R.excl:
                for key, (val, src) in R.r.items():
                    self._need(eng, waits, (key, val, src), False)
        for R in writes:
            if R.w is not None and not (dma is not None and R.w[0] == dma.key):
                self._need(eng, waits, R.w, False)
            for key, (val, src) in R.r.items():
                self._need(eng, waits, (key, val, src), False)
        for key, val in waits.items():
            self.waited[eng][key] = val
        if dma is None:
            self.tick[eng] += 1
            ev = (self._ekey(eng), self.tick[eng], eng)
            inc = (self._ekey(eng), 1)
        else:
            dma.count += 16
            ev = (dma.key, dma.count, "dma")
            inc = (dma.key, 16)
        self.ops[eng].append((tuple(waits.items()), fn, inc))
        self.nins += 1
        for R in reads:
            old = R.r.get(ev[0])
            if old is None or old[0] < ev[1]:
                R.r[ev[0]] = (ev[1], ev[2])
        for R in writes:
            R.w = ev
            R.r = {}
        return ev

    def barrier(self):
        for eng in self.ENGS:
            waits = {}
            for e2 in self.ENGS:
                if e2 != eng and self.tick[e2] > 0:
                    self._need(eng, waits, (self._ekey(e2), self.tick[e2], e2), True)
            for d in self.dmasems:
                if d.count > 0:
                    self._need(eng, waits, (d.key, d.count, "dma"), True)
            self.ops[eng].append((tuple(waits.items()), None, None))
        self._new_epoch()
        for eng in self.ENGS:
            for d in self.dmasems:
                if d.count > 0:
                    self.waited[eng][d.key] = d.count

    def emit(self):
        nc = self.nc
        sems = self.sems

        def run(handle, ops):
            for waits, fn, inc in ops:
                for key, val in waits:
                    handle.wait_ge(sems[key], val)
                if isinstance(fn, tuple):
                    handle.sem_clear(sems[fn[1]])
                elif fn is not None:
                    ins = fn(handle)
                    ins.then_inc(sems[inc[0]], inc[1])

        ops = self.ops
        with nc.Block() as block:
            @block.sync
            def _(h):
                run(h, ops["sp"])

            @block.tensor
            def _(h):
                run(h, ops["pe"])

            @block.scalar
            def _(h):
                run(h, ops["act"])

            @block.vector
            def _(h):
                run(h, ops["dve"])

            @block.gpsimd
            def _(h):
                run(h, ops["pool"])
        self.ops = {e: [] for e in self.ENGS}


class Buf:
    __slots__ = ("t", "res", "sem")

    def __init__(self, t, res, sem=None):
        self.t = t
        self.res = res
        self.sem = sem


class Ctx:
    def __init__(self, nc, S):
        self.nc = nc
        self.S = S
        self.uid = 0
        self.rr = 0
        self.recycle = False
        self.phase_sems = []

    def sb(self, st, shape, dt, dma=False, name=None):
        self.uid += 1
        t = st.enter_context(self.nc.sbuf_tensor("%s_%d" % (name or "sb", self.uid), list(shape), dt))
        sem = self.S.dma_sem() if dma else None
        if sem is not None and self.recycle:
            self.phase_sems.append(sem)
        return Buf(t, Res(name or "sb"), sem)

    def end_phase(self):
        self.S.free.extend(self.phase_sems)
        self.phase_sems = []

    def ps(self, st, shape, dt, name=None):
        self.uid += 1
        t = st.enter_context(self.nc.psum_tensor("%s_%d" % (name or "ps", self.uid), list(shape), dt))
        return Buf(t, Res(name or "ps", excl=True))

    def mm(self, out, lhsT, rhs, start, stop, reads, writes, nocheck=False):
        kw = {"skip_group_check": True} if nocheck else {}
        self.S.op("pe", lambda e: e.matmul(out=out, lhsT=lhsT, rhs=rhs, start=start, stop=stop, **kw),
                  [b.res for b in reads], [b.res for b in writes])

    def tr(self, out, in_, ident, reads, writes):
        self.S.op("pe", lambda e: e.transpose(out=out, in_=in_, identity=ident),
                  [b.res for b in reads], [b.res for b in writes])

    def act(self, out, in_, func, reads, writes, scale=None, bias=None, accum_out=None):
        kw = {}
        if scale is not None:
            kw["scale"] = scale
        if bias is not None:
            kw["bias"] = bias
        if accum_out is not None:
            kw["accum_out"] = accum_out
        self.S.op("act", lambda e: e.activation(out=out, in_=in_, func=func, **kw),
                  [b.res for b in reads], [b.res for b in writes])

    def ts(self, eng, out, in0, s1, s2, op0, op1, reads, writes, accum_out=None):
        kw = {}
        if op1 is not None:
            kw["op1"] = op1
        if accum_out is not None:
            kw["accum_out"] = accum_out
        self.S.op(eng, lambda e: e.tensor_scalar(out=out, in0=in0, scalar1=s1, scalar2=s2, op0=op0, **kw),
                  [b.res for b in reads], [b.res for b in writes])

    def tt(self, eng, out, in0, in1, op, reads, writes):
        self.S.op(eng, lambda e: e.tensor_tensor(out=out, in0=in0, in1=in1, op=op),
                  [b.res for b in reads], [b.res for b in writes])

    def stt(self, out, in0, scalar, in1, op0, op1, reads, writes):
        self.S.op("dve", lambda e: e.scalar_tensor_tensor(out=out, in0=in0, scalar=scalar, in1=in1, op0=op0, op1=op1),
                  [b.res for b in reads], [b.res for b in writes])

    def cp(self, eng, out, in_, reads, writes):
        if eng == "act":
            self.S.op("act", lambda e: e.copy(out=out, in_=in_), [b.res for b in reads], [b.res for b in writes])
        else:
            self.S.op(eng, lambda e: e.tensor_copy(out=out, in_=in_), [b.res for b in reads], [b.res for b in writes])

    def cp_rr(self, out, in_, reads, writes):
        self.rr += 1
        self.cp("act" if self.rr % 2 else "dve", out, in_, reads, writes)

    def memset(self, eng, out, val, writes):
        self.S.op(eng, lambda e: e.memset(out, val), [], [b.res for b in writes])

    def recip(self, out, in_, reads, writes):
        self.S.op("dve", lambda e: e.reciprocal(out=out, in_=in_), [b.res for b in reads], [b.res for b in writes])

    def reduce(self, out, in_, op, reads, writes, absv=False):
        kw = {"apply_absolute_value": True} if absv else {}
        self.S.op("dve", lambda e: e.tensor_reduce(out=out, in_=in_, axis=AX.X, op=op, **kw),
                  [b.res for b in reads], [b.res for b in writes])

    def load(self, out, in_, dst, src_res=()):
        self.S.op("sp", lambda e: e.dma_start(out=out, in_=in_), list(src_res), [dst.res], dma=dst.sem)

    def store(self, out, in_, src, dst_res):
        self.S.op("pool", lambda e: e.dma_start(out=out, in_=in_), [src.res], list(dst_res), dma=src.sem)


def _rel_bucket_np(rel):
    half = 16
    max_exact = 8
    ret = np.where(rel > 0, half, 0)
    n = np.abs(rel)
    nf = np.maximum(n, 1).astype(np.float32)
    large = max_exact + (np.log(nf / np.float32(max_exact)) / np.float32(math.log(1024 / max_exact))
                         * np.float32(half - max_exact)).astype(np.int32)
    large = np.minimum(large, half - 1)
    return ret + np.where(n < max_exact, n, large)


def _tile_info(d):
    s = np.arange(128)[:, None] + 128 * d
    t = np.arange(128)[None, :]
    rel = (s - t).astype(np.int32)
    limit = (t // 64 + 1) * 64
    adm = s < limit
    return _rel_bucket_np(rel), adm


_PAIRS = None


def _pairs():
    global _PAIRS
    if _PAIRS is None:
        pairs = []
        for n in range(NNEAR):
            bs = set()
            for j in (0, 1):
                d = 1 - n - j
                bk, adm = _tile_info(d)
                bs |= set(np.unique(bk[adm]).tolist())
            for b in sorted(bs):
                pairs.append((n, int(b)))
        _PAIRS = pairs
    return _PAIRS


def _tables(j):
    pairs = _pairs()
    ind = np.zeros((128, len(pairs), 128), np.float32)
    for pi, (n, b) in enumerate(pairs):
        d = 1 - n - j
        bk, adm = _tile_info(d)
        ind[:, pi, :] = ((bk == b) & adm)
    pen = np.zeros((128, 256), np.float32)
    for c, n in ((0, 1), (1, 0)):
        d = 1 - n - j
        _, adm = _tile_info(d)
        pen[:, c * 128:(c + 1) * 128] = np.where(adm.T, 0.0, NEG)
    invc = np.zeros((128, 4, 128), np.float32)
    tpos = np.arange(128) + 128 * j
    for g, w in enumerate((2, 4, 8, 16)):
        invc[:, g, :] = (1.0 / np.minimum(w, tpos + 1))[None, :]
    return ind.astype(ml_dtypes.bfloat16), pen, invc


def build_program(debug=False, phases=('pro', 'K', 'P', 'A1', 'A2')):
    nc = bass.Bass("TRN2", target_bir_lowering=False)
    npairs = len(_pairs())

    def din(name, shape, dt=F32):
        return nc.dram_tensor(name, list(shape), dt, kind="ExternalInput").ap()

    def dscr(name, shape, dt):
        return nc.dram_tensor(name, list(shape), dt, kind="ExternalOutput" if (debug and name in debug) else "Internal").ap()

    xfull = din("xfull", [NT, D])
    LW = []
    for l in range(2):
        LW.append(dict(
            w_in=din("w_in%d" % l, [D, INC]), norm_g=din("norm_g%d" % l, [128, 8]),
            pool_w=din("pool_w%d" % l, [4, 128, 128]), pool_scale=din("pool_scale%d" % l, [128, 4]),
            mem_g=din("mem_g%d" % l, [128, 8]), w_mkv=din("w_mkv%d" % l, [D, 1024]),
            w_br=din("w_br%d" % l, [3, 512, D]), w_out=din("w_out%d" % l, [D, D])))
    mem = din("mem", [256, D])
    relb = din("relb", [1, 256])
    fin_g = din("fin_g", [1, D])
    ident_d = din("ident", [128, 128], BF16)
    TAB = {}
    for tb in ("0", "1", "o"):
        TAB[tb] = dict(ind_d=din("ind" + tb, [128, npairs, 128], BF16), pen_d=din("pen" + tb, [128, 256]),
                       invc_d=din("invc" + tb, [128, 4, 128]), blend_d=din("blend" + tb, [128, 2]))
    x1full = nc.dram_tensor("x1full", [NT, D], F32, kind="Internal").ap()
    out_d = nc.dram_tensor("out", [NOWN, D], F32, kind="ExternalOutput").ap()

    k_scr = dscr("k_scr", [128, 4, NT], BF16)
    ik_scr = dscr("ik_scr", [128, NT], BF16)
    v_scr = dscr("v_scr", [NT, 8 * 65], BF16)
    u_scr = dscr("u_scr", [128, 4, 16 + NT], F32)
    q_scr = dscr("q_scr", [128, 4, NOWN], BF16)
    iq_scr = dscr("iq_scr", [128, 4, NOWN], BF16)
    mq_scr = dscr("mq_scr", [128, 4, NOWN], BF16)
    sz_scr = dscr("sz_scr", [128, 4, NOWN], F32)
    saz_scr = dscr("saz_scr", [NOWN, 512], F32)
    smz_scr = dscr("smz_scr", [NOWN, 512], F32)
    sg_scr = dscr("sg_scr", [NOWN, 3072], F32)
    iw_scr = dscr("iw_scr", [NOWN, 8], F32)
    xo_scr = dscr("xo_scr", [NOWN, D], F32)
    y_scr = dscr("y_scr", [128, 3, 4, NOWN], BF16)

    with ExitStack() as top:
        S = Sched(nc, top)
        C = Ctx(nc, S)
        scr_res = {}

        def sres(name, idx):
            k = (name, idx)
            if k not in scr_res:
                scr_res[k] = Res(name)
            return scr_res[k]

        ident = C.sb(top, [128, 128], BF16, dma=True, name="ident")
        EM = C.sb(top, [128, NNEAR, 8, 128], BF16, name="EM")
        mkT = C.sb(top, [128, 4, 256], BF16, name="mkT")
        mv = C.sb(top, [128, 2, 4, 129], BF16, name="mv")
        poolW = C.sb(top, [128, 4, 128], BF16, name="poolW")
        pscale = C.sb(top, [128, 4], F32, dma=True, name="pscale")
        blend = C.sb(top, [128, 2], F32, dma=True, name="blend")
        invc0 = C.sb(top, [128, 4, 128], F32, dma=True, name="invc0")
        invcc = C.sb(top, [128, 4, 128], F32, name="invcc")
        pen = C.sb(top, [128, 256], F32, dma=True, name="pen")
        ng = C.sb(top, [128, 8], F32, dma=True, name="ng")
        mg_ = C.sb(top, [128, 8], F32, dma=True, name="mg")
        halfpow = C.sb(top, [128, NBIS], F32, name="halfpow")

        C.load(ident.t[:], ident_d[:, :], ident)
        C.recycle = True

        def emit_pass(xsrc, w_in, norm_g, pool_w, pool_scale, mem_g, w_mkv, w_br, w_out,
                      ind_d, pen_d, invc_d, blend_d, runK, final, dst_rows):
            C.load(pscale.t[:], pool_scale[:, :], pscale)
            C.load(blend.t[:], blend_d[:, :], blend)
            C.load(invc0.t[:], invc_d[:, :, :], invc0)
            C.load(pen.t[:], pen_d[:, :], pen)
            C.load(ng.t[:], norm_g[:, :], ng)
            C.load(mg_.t[:], mem_g[:, :], mg_)
            for g, w in enumerate((2, 4, 8, 16)):
                C.memset("dve", invcc.t[:, g, :], 1.0 / w, [invcc])
            for k in range(NBIS):
                C.memset("dve", halfpow.t[:, k:k + 1], 2.0 ** (-k), [halfpow])

            with ExitStack() as st:
              if 'pro' in phases:
                indt = C.sb(st, [128, npairs, 128], BF16, dma=True, name="indt")
                rb = C.sb(st, [128, 32, 8], F32, dma=True, name="rb")
                eb = C.sb(st, [128, 32, 8], F32, name="eb")
                pwst = C.sb(st, [128, 4, 128], F32, dma=True, name="pwst")
                C.load(indt.t[:], ind_d[:, :, :], indt)
                C.load(rb.t[:].rearrange("p b h -> p (b h)"), relb[0:1, :].partition_broadcast(128), rb)
                C.load(pwst.t[:], pool_w.rearrange("g c d -> c g d"), pwst)
                C.cp("dve", poolW.t[:], pwst.t[:], [pwst], [poolW])
                C.tt("dve", eb.t[:], rb.t[:], rb.t[:, 15:16, :].broadcast_to([128, 32, 8]), ALU.subtract, [rb], [eb])
                C.act(eb.t[:], eb.t[:], AF.Exp, [eb], [eb])
                C.memset("dve", EM.t[:], 0.0, [EM])
                for pi, (n, b) in enumerate(_pairs()):
                    for h in range(8):
                        C.stt(EM.t[:, n, h, :], indt.t[:, pi, :], eb.t[:, b, h:h + 1], EM.t[:, n, h, :],
                              ALU.mult, ALU.add, [indt, eb, EM], [EM])

                wm = C.sb(st, [128, 8, 1024], BF16, name="wm")
                wst = [C.sb(st, [128, 1024], F32, dma=True, name="wst") for _ in range(2)]
                for kc in range(8):
                    b_ = wst[kc % 2]
                    C.load(b_.t[:], w_mkv[kc * 128:(kc + 1) * 128, :], b_)
                    C.ts("dve" if kc % 2 else "pool", wm.t[:, kc, :], b_.t[:], mg_.t[:, kc:kc + 1], None, ALU.mult, None,
                         [b_, mg_], [wm])
                mt_ = C.sb(st, [128, 2, 1024], F32, dma=True, name="memt")
                C.load(mt_.t[:], mem.rearrange("(a p) d -> p a d", p=128), mt_)
                sqj = C.sb(st, [128, 1024], BF16, name="sqj")
                ss = C.sb(st, [128, 2], F32, name="ss")
                memn = C.sb(st, [128, 2, 1024], BF16, name="memn")
                memnT = C.sb(st, [128, 8, 256], BF16, name="memnT")
                for a in range(2):
                    C.act(sqj.t[:], mt_.t[:, a, :], AF.Square, [mt_], [sqj, ss], accum_out=ss.t[:, a:a + 1])
                C.act(ss.t[:], ss.t[:], AF.Sqrt, [ss], [ss], scale=1.0 / D, bias=EPS)
                C.recip(ss.t[:], ss.t[:], [ss], [ss])
                for a in range(2):
                    C.act(memn.t[:, a, :], mt_.t[:, a, :], AF.Copy, [mt_, ss], [memn], scale=ss.t[:, a:a + 1])
                ptr = C.ps(st, [128, 1024], BF16, name="ptr")
                pmm = [C.ps(st, [128, 512], F32, name="pmm") for _ in range(2)]
                for kc in range(8):
                    for a in range(2):
                        C.tr(ptr.t[:, a * 128:(a + 1) * 128], memn.t[:, a, kc * 128:(kc + 1) * 128], ident.t[:],
                             [memn, ident], [ptr])
                    C.cp_rr(memnT.t[:, kc, :], ptr.t[:, 0:256], [ptr], [memnT])
                for h in range(4):
                    p_ = pmm[h % 2]
                    for kc in range(8):
                        C.mm(p_.t[:, 0:256], wm.t[:, kc, h * 128:(h + 1) * 128], memnT.t[:, kc, :], kc == 0, kc == 7,
                             [wm, memnT], [p_])
                    C.cp_rr(mkT.t[:, h, :], p_.t[:, 0:256], [p_], [mkT])
                C.memset("dve", mv.t[:], 1.0, [mv])
                for a in range(2):
                    p_ = pmm[a % 2]
                    for kc in range(8):
                        C.mm(p_.t[:], memnT.t[:, kc, a * 128:(a + 1) * 128], wm.t[:, kc, 512:1024], kc == 0, kc == 7,
                             [wm, memnT], [p_])
                    C.cp_rr(mv.t[:, a, :, 0:128], p_.t[:].rearrange("p (h d) -> p h d", h=4), [p_], [mv])
                S.barrier()
                S.emit()
                C.end_phase()

            with ExitStack() as st:
              if 'K' in phases and runK:
                NK = 1664
                wk = C.sb(st, [128, 8, NK], BF16, name="wk")
                wst = [C.sb(st, [128, NK], F32, dma=True, name="wstk") for _ in range(2)]
                for kc in range(8):
                    b_ = wst[kc % 2]
                    rows = slice(kc * 128, (kc + 1) * 128)
                    C.load(b_.t[:, 0:512], w_in[rows, C_AK:C_AK + 512], b_)
                    C.load(b_.t[:, 512:576], w_in[rows, C_IK:C_IK + 64], b_)
                    C.load(b_.t[:, 576:640], w_in[rows, C_IK:C_IK + 64], b_)
                    C.load(b_.t[:, 640:1152], w_in[rows, C_PU:C_PU + 512], b_)
                    C.load(b_.t[:, 1152:1664], w_in[rows, C_AV:C_AV + 512], b_)
                    C.ts("dve" if kc % 2 else "pool", wk.t[:, kc, :], b_.t[:], ng.t[:, kc:kc + 1], None, ALU.mult, None,
                         [b_, ng], [wk])
                zt = C.sb(st, [128, 4, 16], F32, dma=True, name="zt")
                C.memset("dve", zt.t[:], 0.0, [zt])
                C.store(u_scr[:, :, 0:16], zt.t[:], zt, [sres("u", -1)])
                xt = [C.sb(st, [128, 4, D], F32, dma=True, name="xt") for _ in range(2)]
                sqj = C.sb(st, [128, D], BF16, name="sqj")
                ss = [C.sb(st, [128, 4], F32, name="ss") for _ in range(2)]
                xn = [C.sb(st, [128, 4, D], BF16, name="xn") for _ in range(2)]
                xnT = [C.sb(st, [128, 8, 512], BF16, name="xnT") for _ in range(2)]
                kst = [C.sb(st, [128, 4, 512], BF16, dma=True, name="kst") for _ in range(2)]
                ikst = [C.sb(st, [128, 512], BF16, dma=True, name="ikst") for _ in range(2)]
                ust = [C.sb(st, [128, 4, 512], F32, dma=True, name="ust") for _ in range(2)]
                vst = [C.sb(st, [128, 4, 8, 65], BF16, dma=True, name="vst") for _ in range(2)]
                for b_ in vst:
                    C.memset("dve", b_.t[:], 1.0, [b_])
                ptr = [C.ps(st, [128, 1024], BF16, name="ptr") for _ in range(2)]
                pf = [C.ps(st, [128, 512], F32, name="pf") for _ in range(2)]
                pv = [C.ps(st, [128, 512], F32, name="pv") for _ in range(2)]
                for T in range(16):
                    x_ = xt[T % 2]
                    s_ = ss[T % 2]
                    n_ = xn[T % 2]
                    nT = xnT[T % 2]
                    C.load(x_.t[:], xsrc[T * 512:(T + 1) * 512, :].rearrange("(a p) d -> p a d", p=128), x_,
                           [sres("xfull", T)])
                    for a in range(4):
                        C.act(sqj.t[:], x_.t[:, a, :], AF.Square, [x_], [sqj, s_], accum_out=s_.t[:, a:a + 1])
                    C.act(s_.t[:], s_.t[:], AF.Sqrt, [s_], [s_], scale=1.0 / D, bias=EPS)
                    C.recip(s_.t[:], s_.t[:], [s_], [s_])
                    for a in range(4):
                        if a % 2 == 0:
                            C.act(n_.t[:, a, :], x_.t[:, a, :], AF.Copy, [x_, s_], [n_], scale=s_.t[:, a:a + 1])
                        else:
                            C.ts("dve", n_.t[:, a, :], x_.t[:, a, :], s_.t[:, a:a + 1], None, ALU.mult, None, [x_, s_], [n_])
                    for kc in range(8):
                        p_ = ptr[kc % 2]
                        for a in range(4):
                            C.tr(p_.t[:, a * 128:(a + 1) * 128], n_.t[:, a, kc * 128:(kc + 1) * 128], ident.t[:],
                                 [n_, ident], [p_])
                        C.cp_rr(nT.t[:, kc, :], p_.t[:, 0:512], [p_], [nT])
                    ks, iks, us, vs = kst[T % 2], ikst[T % 2], ust[T % 2], vst[T % 2]
                    for ct in range(9):
                        p_ = pf[ct % 2]
                        for kc in range(8):
                            C.mm(p_.t[:], wk.t[:, kc, ct * 128:(ct + 1) * 128], nT.t[:, kc, :], kc == 0, kc == 7,
                                 [wk, nT], [p_])
                        if ct < 4:
                            C.cp_rr(ks.t[:, ct, :], p_.t[:], [p_], [ks])
                        elif ct == 4:
                            C.cp_rr(iks.t[:], p_.t[:], [p_], [iks])
                        else:
                            C.cp_rr(us.t[:, ct - 5, :], p_.t[:], [p_], [us])
                    C.store(k_scr[:, :, T * 512:(T + 1) * 512], ks.t[:], ks, [sres("k", T)])
                    C.store(ik_scr[:, T * 512:(T + 1) * 512], iks.t[:], iks, [sres("ik", T)])
                    C.store(u_scr[:, :, 16 + T * 512:16 + (T + 1) * 512], us.t[:], us, [sres("u", T)])
                    for a in range(4):
                        p_ = pv[a % 2]
                        for kc in range(8):
                            C.mm(p_.t[:], nT.t[:, kc, a * 128:(a + 1) * 128], wk.t[:, kc, 1152:1664], kc == 0, kc == 7,
                                 [wk, nT], [p_])
                        C.cp_rr(vs.t[:, a, :, 0:64], p_.t[:].rearrange("p (h d) -> p h d", h=8), [p_], [vs])
                    C.store(v_scr[T * 512:(T + 1) * 512, :].rearrange("(a p) c -> p a c", p=128),
                            vs.t[:].rearrange("p a h c -> p a (h c)"), vs, [sres("v", T)])
                S.barrier()
                S.emit()
                C.end_phase()

            with ExitStack() as st:
              if 'P' in phases:
                NF = 2048
                NTM = 4104
                NP = NF + NTM
                wp = C.sb(st, [128, 8, NP], BF16, name="wp")
                tmst = [C.sb(st, [128, 2048], F32, dma=True, name="tmst") for _ in range(2)]
                cnt = 0
                for kc in range(8):
                    rows = slice(kc * 128, (kc + 1) * 128)
                    groups = [
                        (0, [(C_PZ, 512), (C_AQ, 512), (C_IQ, 512), (C_MQ, 512)]),
                        (2048, [(C_AZ, 512), (C_MZ, 512), (C_G, 1024)]),
                        (4096, [(C_G + 1024, 2048)]),
                        (6144, [(C_IW, 8)]),
                    ]
                    for (dst0, parts) in groups:
                        b_ = tmst[cnt % 2]
                        o = 0
                        for (c0, n) in parts:
                            C.load(b_.t[:, o:o + n], w_in[rows, c0:c0 + n], b_)
                            o += n
                        C.ts("dve" if cnt % 2 else "pool", wp.t[:, kc, dst0:dst0 + o], b_.t[:, 0:o], ng.t[:, kc:kc + 1],
                             None, ALU.mult, None, [b_, ng], [wp])
                        cnt += 1
                cand = [C.sb(st, [128, 4, D], F32, dma=True, name="cand") for _ in range(1)]
                xo = [C.sb(st, [128, 2, D], F32, dma=True, name="xo") for _ in range(1)]
                sqj = C.sb(st, [128, D], BF16, name="sqj")
                ss = [C.sb(st, [128, 2], F32, name="ss") for _ in range(2)]
                xn = C.sb(st, [128, 2, D], BF16, name="xn")
                xnT = [C.sb(st, [128, 8, 256], BF16, name="xnT") for _ in range(1)]
                szst = [C.sb(st, [128, 4, 256], F32, dma=True, name="szst") for _ in range(1)]
                fmst = [[C.sb(st, [128, 4, 256], BF16, dma=True, name="fmst") for _ in range(1)] for _ in range(3)]
                iwst = [C.sb(st, [128, 8], F32, dma=True, name="iwst") for _ in range(2)]
                ptr = [C.ps(st, [128, 1024], BF16, name="ptr") for _ in range(2)]
                pf = [C.ps(st, [128, 512], F32, name="pf") for _ in range(2)]
                pt = [C.ps(st, [128, 512], F32, name="pt") for _ in range(3)]
                for U in range(16):
                    c_ = cand[0]
                    x_ = xo[0]
                    s_ = ss[U % 2]
                    nT = xnT[0]
                    C.load(c_.t[:], xsrc[U * 512:(U + 1) * 512, :].rearrange("(a p) d -> p a d", p=128), c_,
                           [sres("xfull", U)])
                    C.ts("dve", x_.t[:], c_.t[:, 0:4:2, :], blend.t[:, 0:1], None, ALU.mult, None, [c_, blend], [x_])
                    C.stt(x_.t[:], c_.t[:, 1:4:2, :], blend.t[:, 1:2], x_.t[:], ALU.mult, ALU.add, [c_, blend, x_], [x_])
                    C.store(xo_scr[U * 256:(U + 1) * 256, :].rearrange("(a p) d -> p a d", p=128), x_.t[:], x_,
                            [sres("xo", U)])
                    for a in range(2):
                        C.act(sqj.t[:], x_.t[:, a, :], AF.Square, [x_], [sqj, s_], accum_out=s_.t[:, a:a + 1])
                    C.act(s_.t[:], s_.t[:], AF.Sqrt, [s_], [s_], scale=1.0 / D, bias=EPS)
                    C.recip(s_.t[:], s_.t[:], [s_], [s_])
                    C.act(xn.t[:, 0, :], x_.t[:, 0, :], AF.Copy, [x_, s_], [xn], scale=s_.t[:, 0:1])
                    C.ts("dve", xn.t[:, 1, :], x_.t[:, 1, :], s_.t[:, 1:2], None, ALU.mult, None, [x_, s_], [xn])
                    for kc in range(8):
                        p_ = ptr[kc % 2]
                        for a in range(2):
                            C.tr(p_.t[:, a * 128:(a + 1) * 128], xn.t[:, a, kc * 128:(kc + 1) * 128], ident.t[:],
                                 [xn, ident], [p_])
                        C.cp_rr(nT.t[:, kc, :], p_.t[:, 0:256], [p_], [nT])
                    szs = szst[0]
                    fms = [fmst[k][0] for k in range(3)]
                    for ct in range(16):
                        p_ = pf[ct % 2]
                        for kc in range(8):
                            C.mm(p_.t[:, 0:256], wp.t[:, kc, ct * 128:(ct + 1) * 128], nT.t[:, kc, :], kc == 0, kc == 7,
                                 [wp, nT], [p_])
                        if ct < 4:
                            C.act(szs.t[:, ct, :], p_.t[:, 0:256], AF.Silu, [p_], [szs])
                        else:
                            f_ = fms[ct // 4 - 1]
                            C.cp("dve", f_.t[:, ct % 4, :], p_.t[:, 0:256], [p_], [f_])
                    col = slice(U * 256, (U + 1) * 256)
                    C.store(sz_scr[:, :, col], szs.t[:], szs, [sres("sz", U)])
                    C.store(q_scr[:, :, col], fms[0].t[:], fms[0], [sres("q", U)])
                    C.store(iq_scr[:, :, col], fms[1].t[:], fms[1], [sres("iq", U)])
                    C.store(mq_scr[:, :, col], fms[2].t[:], fms[2], [sres("mq", U)])
                    for a in range(2):
                        iws = iwst[a]
                        rows = slice(U * 256 + a * 128, U * 256 + (a + 1) * 128)
                        for grp in range(8):
                            tm = tmst[grp // 4]
                            p_ = pt[grp % 3]
                            for kc in range(8):
                                C.mm(p_.t[:], nT.t[:, kc, a * 128:(a + 1) * 128],
                                     wp.t[:, kc, NF + grp * 512:NF + (grp + 1) * 512], kc == 0, kc == 7, [wp, nT], [p_])
                            C.act(tm.t[:, (grp % 4) * 512:(grp % 4 + 1) * 512], p_.t[:], AF.Silu if grp < 2 else AF.Sigmoid,
                                  [p_], [tm])
                            if grp == 3:
                                C.store(saz_scr[rows, :], tm.t[:, 0:512], tm, [sres("saz", 2 * U + a)])
                                C.store(smz_scr[rows, :], tm.t[:, 512:1024], tm, [sres("smz", 2 * U + a)])
                                C.store(sg_scr[rows, 0:1024], tm.t[:, 1024:2048], tm, [sres("sg", 2 * U + a)])
                            if grp == 7:
                                C.store(sg_scr[rows, 1024:3072], tm.t[:, 0:2048], tm, [sres("sg", 2 * U + a)])
                        p_ = pt[2]
                        for kc in range(8):
                            C.mm(p_.t[:, 0:8], nT.t[:, kc, a * 128:(a + 1) * 128], wp.t[:, kc, NF + 4096:NF + 4104],
                                 kc == 0, kc == 7, [wp, nT], [p_])
                        C.ts("dve", iws.t[:], p_.t[:, 0:8], IDX_W_SCALE, None, ALU.mult, None, [p_], [iws])
                        C.store(iw_scr[rows, :], iws.t[:], iws, [sres("iw", 2 * U + a)])
                S.barrier()
                S.emit()
                C.end_phase()

            with ExitStack() as st:
              if 'A1' in phases:
                scores = C.sb(st, [128, NT], F32, name="scores")
                mask01 = [C.sb(st, [128, NT], BF16, name="mask01") for _ in range(2)]
                mT = [C.sb(st, [128, 4, 128], BF16, name="mT") for _ in range(2)]
                kTc = [C.sb(st, [128, 4, 512], BF16, dma=True, name="kTc") for _ in range(2)]
                Vc = [C.sb(st, [128, 4, 520], BF16, dma=True, name="Vc") for _ in range(2)]
                ikt = [C.sb(st, [128, 512], BF16, dma=True, name="ikt") for _ in range(2)]
                Rt = [[C.sb(st, [128, 512], BF16, name="Rt") for _ in range(8)] for _ in range(2)]
                Et = [C.sb(st, [128, 4, 128], BF16, name="Et") for _ in range(2)]
                Et2 = [C.sb(st, [128, 4, 128], BF16, name="Et2") for _ in range(2)]
                Pt = [C.sb(st, [128, 4, 128], BF16, name="Pt") for _ in range(2)]
                qT = [C.sb(st, [128, 4, 128], BF16, dma=True, name="qT") for _ in range(2)]
                iqT = [C.sb(st, [128, 4, 128], BF16, dma=True, name="iqT") for _ in range(2)]
                mqT = [C.sb(st, [128, 4, 128], BF16, dma=True, name="mqT") for _ in range(2)]
                iw = [C.sb(st, [128, 8], F32, dma=True, name="iw") for _ in range(2)]
                qz = [C.sb(st, [128, 8, 128], BF16, name="qz") for _ in range(2)]
                iqz = [C.sb(st, [128, 8, 128], BF16, name="iqz") for _ in range(2)]
                for b_ in qz + iqz:
                    C.memset("dve", b_.t[:], 0.0, [b_])
                Dg = [C.sb(st, [128, 8, 128], BF16, name="Dg") for _ in range(2)]
                szT = [C.sb(st, [128, 4, 128], F32, dma=True, name="szT") for _ in range(2)]
                saz = [C.sb(st, [128, 512], F32, dma=True, name="saz") for _ in range(2)]
                smz = [C.sb(st, [128, 512], F32, dma=True, name="smz") for _ in range(2)]
                ua = [C.sb(st, [128, 4, 144], F32, dma=True, name="ua") for _ in range(2)]
                ub = [C.sb(st, [128, 4, 144], F32, dma=True, name="ub") for _ in range(2)]
                uw = C.sb(st, [128, 4, 144], F32, name="uw")
                s1 = C.sb(st, [128, 4, 144], F32, name="s1")
                s2 = C.sb(st, [128, 4, 144], F32, name="s2")
                s3 = C.sb(st, [128, 4, 144], F32, name="s3")
                Sall = C.sb(st, [128, 4, 128], F32, name="Sall")
                plb = C.sb(st, [128, 4, 128], BF16, name="plb")
                ypf = C.sb(st, [128, 4, 128], F32, name="ypf")
                amax = C.sb(st, [128, 20], F32, name="amax")
                A_ = C.sb(st, [128, 1], F32, name="A")
                steps = C.sb(st, [128, NBIS], F32, name="steps")
                lo = C.sb(st, [128, 1], F32, name="lo")
                cc = C.sb(st, [128, 1], F32, name="cc")
                cnt_ = C.sb(st, [128, 1], F32, name="cnt")
                dd = C.sb(st, [128, 1], F32, name="dd")
                rs = C.sb(st, [128, 8], F32, name="rs")
                accs = [C.sb(st, [128, 4, 65], F32, name="accs") for _ in range(2)]
                yaf = C.sb(st, [128, 8, 64], F32, name="yaf")
                yab = C.sb(st, [128, 512], BF16, name="yab")
                rsm = C.sb(st, [128, 4], F32, name="rsm")
                ymf = C.sb(st, [128, 4, 128], F32, name="ymf")
                ymb = C.sb(st, [128, 512], BF16, name="ymb")
                Pm = C.sb(st, [128, 8, 128], BF16, name="Pm")
                yst = [C.sb(st, [128, 3, 4, 128], BF16, dma=True, name="yst") for _ in range(2)]
                psc = [C.ps(st, [128, 512], F32, name="psc") for _ in range(2)]
                pidx = C.ps(st, [128, 512], F32, name="pidx")
                pmT = C.ps(st, [128, 1024], BF16, name="pmT")
                pl = [C.ps(st, [128, 4, 128], F32, name="pl") for _ in range(2)]
                pacc = [C.ps(st, [128, 512], F32, name="pacc") for _ in range(2)]

                NS = int(os.environ.get('A1_SLOTS', NSLOT))
                MASKENG = os.environ.get('MASKENG', 'dve')

                def idx_tiles(i):
                    L = (2 * i + 2) * 128
                    tiles = []
                    k0 = 0
                    if (L // 256) % 2 == 1:
                        tiles.append((0, 256))
                        k0 = 256
                    while k0 < L:
                        tiles.append((k0, 512))
                        k0 += 512
                    return tiles

                def prep(i):
                    par = i % 2
                    col = slice(i * 128, (i + 1) * 128)
                    rows = slice(i * 128, (i + 1) * 128)
                    U = i // 2
                    q_, iq_, mq_, iw_, Dg_ = qT[par], iqT[par], mqT[par], iw[par], Dg[par]
                    sz_, saz_, smz_, ua_, ub_ = szT[par], saz[par], smz[par], ua[par], ub[par]
                    C.load(q_.t[:], q_scr[:, :, col], q_, [sres("q", U)])
                    C.load(iq_.t[:], iq_scr[:, :, col], iq_, [sres("iq", U)])
                    C.load(mq_.t[:], mq_scr[:, :, col], mq_, [sres("mq", U)])
                    C.load(iw_.t[:], iw_scr[rows, :], iw_, [sres("iw", i)])
                    C.load(sz_.t[:], sz_scr[:, :, col], sz_, [sres("sz", U)])
                    C.load(saz_.t[:], saz_scr[rows, :], saz_, [sres("saz", i)])
                    C.load(smz_.t[:], smz_scr[rows, :], smz_, [sres("smz", i)])
                    C.load(ua_.t[:], u_scr[:, :, 256 * i:256 * i + 144], ua_, [sres("u", -1)])
                    C.load(ub_.t[:], u_scr[:, :, 256 * i + 128:256 * i + 272], ub_, [sres("u", -1)])
                    qz_, iqz_ = qz[par], iqz[par]
                    for r in range(2):
                        ps_ = slice(r * 64, (r + 1) * 64)
                        C.cp("dve", qz_.t[ps_, r:8:2, :], q_.t[ps_, :, :], [q_], [qz_])
                        C.cp("dve", iqz_.t[ps_, r:8:2, :], iq_.t[ps_, :, :], [iq_], [iqz_])
                    C.tt("dve", Dg_.t[:], ident.t[:].unsqueeze(1).broadcast_to([128, 8, 128]),
                         iw_.t[:].unsqueeze(2).broadcast_to([128, 8, 128]), ALU.mult, [ident, iw_], [Dg_])

                def gen_idx(i):
                    par = i % 2
                    Dg_, iqz_ = Dg[par], iqz[par]
                    tiles = idx_tiles(i)
                    for ti, (k0, kw) in enumerate(tiles):
                        ik_ = ikt[ti % 2]
                        R_ = Rt[ti % 2]
                        C.load(ik_.t[:, 0:kw], ik_scr[:, k0:k0 + kw], ik_, [sres("ik", k0 // 512)])

                        def sc(h):
                            p_ = psc[h % 2]
                            C.mm(p_.t[:, 0:kw], iqz_.t[:, h, :], ik_.t[:, 0:kw], True, True, [iqz_, ik_], [p_])

                        def relu(h):
                            p_ = psc[h % 2]
                            C.act(R_[h].t[:, 0:kw], p_.t[:, 0:kw], AF.Relu, [p_], [R_[h]])

                        def red(h):
                            C.mm(pidx.t[:, 0:kw], Dg_.t[:, h, :], R_[h].t[:, 0:kw], h == 0, h == 7, [Dg_, R_[h]], [pidx])
                        sc(0)
                        sc(1)
                        for h in range(8):
                            relu(h)
                            if h + 2 < 8:
                                sc(h + 2)
                            red(h)
                            yield
                        C.reduce(amax.t[:, ti:ti + 1], pidx.t[:, 0:kw], ALU.max, [pidx], [amax], absv=True)
                        if ti == len(tiles) - 1:
                            if kw > 256:
                                C.cp("act", scores.t[:, k0:k0 + kw - 256], pidx.t[:, 0:kw - 256], [pidx], [scores])
                            C.tt("dve", scores.t[:, k0 + kw - 256:k0 + kw], pidx.t[:, kw - 256:kw], pen.t[:], ALU.add,
                                 [pidx, pen], [scores])
                        else:
                            C.cp("act", scores.t[:, k0:k0 + kw], pidx.t[:, 0:kw], [pidx], [scores])
                        yield

                def gen_bis(i):
                    L = (2 * i + 2) * 128
                    m01 = mask01[i % 2]
                    C.reduce(A_.t[:], amax.t[:, 0:len(idx_tiles(i))], ALU.max, [amax], [A_])
                    C.ts("dve", A_.t[:], A_.t[:], 1.0001, 1e-20, ALU.mult, ALU.add, [A_], [A_])
                    C.ts("dve", steps.t[:], halfpow.t[:], A_.t[:, 0:1], None, ALU.mult, None, [halfpow, A_], [steps])
                    C.ts("dve", lo.t[:], A_.t[:], -1.0, None, ALU.mult, None, [A_], [lo])
                    yield
                    for k in range(NBIS):
                        C.tt("dve", cc.t[:], lo.t[:], steps.t[:, k:k + 1], ALU.add, [lo, steps], [cc])
                        C.ts("dve", m01.t[:, 0:L], scores.t[:, 0:L], cc.t[:, 0:1], 0.0, ALU.is_ge, ALU.add,
                             [scores, cc], [m01, cnt_], accum_out=cnt_.t[:])
                        C.stt(dd.t[:], cnt_.t[:], TOPK - 0.5, steps.t[:, k:k + 1], ALU.is_ge, ALU.mult,
                              [cnt_, steps], [dd])
                        C.tt("dve", lo.t[:], lo.t[:], dd.t[:], ALU.add, [lo, dd], [lo])
                        yield
                    C.ts("dve", m01.t[:, 0:L], scores.t[:, 0:L], lo.t[:, 0:1], None, ALU.is_ge, None,
                         [scores, lo], [m01])
                    yield

                def gen_att(i):
                    par = i % 2
                    nkt = 2 * i + 2
                    col = slice(i * 128, (i + 1) * 128)
                    mq_, qz_ = mqT[par], qz[par]
                    sz_, saz_, smz_, ua_, ub_ = szT[par], saz[par], smz[par], ua[par], ub[par]
                    m01 = mask01[par]
                    steps_ = [(kt, hg) for kt in range(nkt) for hg in range(2)]

                    def chunk_setup(c):
                        kt0 = c * 4
                        nk = min(4, nkt - kt0)
                        kc_, vc_, mT_ = kTc[c % 2], Vc[c % 2], mT[c % 2]
                        C.load(kc_.t[:, :, 0:nk * 128], k_scr[:, :, kt0 * 128:(kt0 + nk) * 128], kc_,
                               [sres("k", (kt0 * 128) // 512)])
                        C.load(vc_.t[:, 0:nk, :], v_scr[kt0 * 128:(kt0 + nk) * 128, :].rearrange("(a p) c -> p a c", p=128),
                               vc_, [sres("v", (kt0 * 128) // 512)])
                        for jj in range(nk):
                            kt = kt0 + jj
                            C.tr(pmT.t[:, jj * 128:(jj + 1) * 128], m01.t[:, kt * 128:(kt + 1) * 128], ident.t[:],
                                 [m01, ident], [pmT])
                        C.cp("act", mT_.t[:, 0:nk, :], pmT.t[:, 0:nk * 128].rearrange("p (a t) -> p a t", a=nk),
                             [pmT], [mT_])

                    def stA(sidx):
                        kt, hg = steps_[sidx]
                        c, jj = kt // 4, kt % 4
                        if jj == 0 and hg == 0:
                            chunk_setup(c)
                        kc_ = kTc[c % 2]
                        p_ = pl[hg]
                        for hh in range(4):
                            h = hg * 4 + hh
                            C.mm(p_.t[:, hh, :], kc_.t[:, h // 2, jj * 128:(jj + 1) * 128],
                                 qz_.t[:, h, :], True, True, [kc_, qz_], [p_])

                    def stB(sidx):
                        kt, hg = steps_[sidx]
                        c, jj = kt // 4, kt % 4
                        n = nkt - 1 - kt
                        mT_ = mT[c % 2]
                        p_, e_, P_ = pl[hg], Et[hg], Pt[hg]
                        C.act(e_.t[:], p_.t[:], AF.Exp, [p_], [e_], scale=ATTN_SCALE)
                        mb = mT_.t[:, jj:jj + 1, :].broadcast_to([128, 4, 128])
                        if n < NNEAR:
                            e2 = Et2[hg]
                            C.tt("dve", e2.t[:], e_.t[:], EM.t[:, n, hg * 4:(hg + 1) * 4, :], ALU.mult,
                                 [e_, EM], [e2])
                            C.tt(MASKENG, P_.t[:], e2.t[:], mb, ALU.mult, [e2, mT_], [P_])
                        else:
                            C.tt(MASKENG, P_.t[:], e_.t[:], mb, ALU.mult, [e_, mT_], [P_])

                    def stC(sidx):
                        kt, hg = steps_[sidx]
                        c, jj = kt // 4, kt % 4
                        vc_ = Vc[c % 2]
                        P_, a_ = Pt[hg], pacc[hg]
                        for hh in range(4):
                            h = hg * 4 + hh
                            C.mm(a_.t[:, hh * 65:(hh + 1) * 65], P_.t[:, hh, :], vc_.t[:, jj, h * 65:(h + 1) * 65],
                                 kt == 0 and hh == 0, kt == nkt - 1, [P_, vc_], [a_], nocheck=True)

                    stA(0)
                    stA(1)
                    for sidx in range(len(steps_)):
                        stB(sidx)
                        if sidx + 2 < len(steps_):
                            stA(sidx + 2)
                        stC(sidx)
                        yield
                    ys = yst[par]
                    for hg in range(2):
                        C.cp("dve", accs[hg].t[:], pacc[hg].t[:, 0:260].rearrange("p (h c) -> p h c", h=4),
                             [pacc[hg]], [accs[hg]])
                        av = accs[hg].t[:]
                        C.recip(rs.t[:, hg * 4:(hg + 1) * 4], av[:, :, 64], [accs[hg]], [rs])
                        C.tt("dve", yaf.t[:, hg * 4:(hg + 1) * 4, :], av[:, :, 0:64],
                             rs.t[:, hg * 4:(hg + 1) * 4].unsqueeze(2).broadcast_to([128, 4, 64]), ALU.mult,
                             [accs[hg], rs], [yaf])
                    C.tt("dve", yab.t[:], yaf.t[:].rearrange("p h d -> p (h d)"), saz_.t[:], ALU.mult, [yaf, saz_], [yab])
                    for kc in range(4):
                        C.tr(pmT.t[:, kc * 128:(kc + 1) * 128], yab.t[:, kc * 128:(kc + 1) * 128], ident.t[:],
                             [yab, ident], [pmT])
                    C.cp("act", ys.t[:, 1, :, :], pmT.t[:, 0:512].rearrange("p (a t) -> p a t", a=4), [pmT], [ys])
                    yield

                    for mt in range(2):
                        p_ = pl[mt]
                        for hm in range(4):
                            C.mm(p_.t[:, hm, :], mkT.t[:, hm, mt * 128:(mt + 1) * 128], mq_.t[:, hm, :], True, True,
                                 [mkT, mq_], [p_])
                        C.act(Pm.t[:, mt * 4:(mt + 1) * 4, :], p_.t[:], AF.Exp, [p_], [Pm], scale=MEM_SCALE)
                    for hm in range(4):
                        a_ = pacc[hm // 2]
                        o = (hm % 2) * 129
                        for mt in range(2):
                            C.mm(a_.t[:, o:o + 129], Pm.t[:, mt * 4 + hm, :], mv.t[:, mt, hm, :], mt == 0, mt == 1,
                                 [Pm, mv], [a_])
                    for hp in range(2):
                        av = pacc[hp].t[:, 0:258].rearrange("p (h c) -> p h c", h=2)
                        C.recip(rsm.t[:, hp * 2:(hp + 1) * 2], av[:, :, 128], [pacc[hp]], [rsm])
                        C.tt("dve", ymf.t[:, hp * 2:(hp + 1) * 2, :], av[:, :, 0:128],
                             rsm.t[:, hp * 2:(hp + 1) * 2].unsqueeze(2).broadcast_to([128, 2, 128]), ALU.mult,
                             [pacc[hp], rsm], [ymf])
                    C.tt("dve", ymb.t[:], ymf.t[:].rearrange("p h d -> p (h d)"), smz_.t[:], ALU.mult, [ymf, smz_], [ymb])
                    for kc in range(4):
                        C.tr(pmT.t[:, kc * 128:(kc + 1) * 128], ymb.t[:, kc * 128:(kc + 1) * 128], ident.t[:],
                             [ymb, ident], [pmT])
                    C.cp("act", ys.t[:, 2, :, :], pmT.t[:, 0:512].rearrange("p (a t) -> p a t", a=4), [pmT], [ys])
                    yield

                    C.ts("dve", uw.t[:], ua_.t[:], blend.t[:, 0:1], None, ALU.mult, None, [ua_, blend], [uw])
                    C.stt(uw.t[:], ub_.t[:], blend.t[:, 1:2], uw.t[:], ALU.mult, ALU.add, [ub_, blend, uw], [uw])
                    C.tt("dve", s1.t[:, :, 1:144], uw.t[:, :, 1:144], uw.t[:, :, 0:143], ALU.add, [uw], [s1])
                    C.tt("dve", s2.t[:, 1:4, 3:144], s1.t[:, 1:4, 3:144], s1.t[:, 1:4, 1:142], ALU.add, [s1], [s2])
                    C.tt("dve", s3.t[:, 2:4, 7:144], s2.t[:, 2:4, 7:144], s2.t[:, 2:4, 3:140], ALU.add, [s2], [s3])
                    C.cp("dve", Sall.t[:, 0, :], s1.t[:, 0, 16:144], [s1], [Sall])
                    C.cp("dve", Sall.t[:, 1, :], s2.t[:, 1, 16:144], [s2], [Sall])
                    C.cp("dve", Sall.t[:, 2, :], s3.t[:, 2, 16:144], [s3], [Sall])
                    C.tt("dve", Sall.t[:, 3, :], s3.t[:, 3, 16:144], s3.t[:, 3, 8:136], ALU.add, [s3], [Sall])
                    ic = invc0 if i == 0 else invcc
                    C.tt("dve", Sall.t[:], Sall.t[:], ic.t[:], ALU.mult, [Sall, ic], [Sall])
                    C.tt("dve", plb.t[:], Sall.t[:], uw.t[:, :, 16:144], ALU.subtract, [Sall, uw], [plb])
                    pp = pl[1]
                    for g in range(4):
                        C.mm(pp.t[:, g, :], poolW.t[:, g, :], plb.t[:, g, :], True, True, [poolW, plb], [pp])
                    C.tt("dve", ypf.t[:], pp.t[:], pscale.t[:].unsqueeze(2).broadcast_to([128, 4, 128]), ALU.mult,
                         [pp, pscale], [ypf])
                    C.tt("dve", ys.t[:, 0, :, :], ypf.t[:], sz_.t[:], ALU.mult, [ypf, sz_], [ys])
                    C.store(y_scr[:, :, :, col], ys.t[:], ys, [sres("y", i)])
                    yield

                SENT = object()
                prep(0)
                for _ in gen_idx(0):
                    pass
                for _ in gen_bis(0):
                    pass
                for i in range(NS):
                    if i + 1 < NS:
                        prep(i + 1)
                        side = itertools.chain(gen_idx(i + 1), gen_bis(i + 1))
                        n_side = 9 * len(idx_tiles(i + 1)) + NBIS + 2
                    else:
                        side = iter(())
                        n_side = 0
                    n_main = 2 * (2 * i + 2) + 3
                    done = 0
                    for m, _ in enumerate(gen_att(i)):
                        target = ((m + 1) * n_side + n_main - 1) // n_main
                        while done < target:
                            if next(side, SENT) is SENT:
                                done = n_side
                                break
                            done += 1
                    for _ in side:
                        pass
                S.barrier()
                S.emit()
                C.end_phase()

            with ExitStack() as st:
              if 'A2' in phases:
                wb = C.sb(st, [128, 3, 4, D], BF16, name="wb")
                wo = C.sb(st, [128, 8, D], BF16, name="wo")
                wst = [C.sb(st, [128, D], F32, dma=True, name="wsta") for _ in range(2)]
                cnt = 0
                for br in range(3):
                    for kc in range(4):
                        b_ = wst[cnt % 2]
                        C.load(b_.t[:], w_br[br, kc * 128:(kc + 1) * 128, :], b_)
                        C.cp("dve" if cnt % 2 else "pool", wb.t[:, br, kc, :], b_.t[:], [b_], [wb])
                        cnt += 1
                for kc in range(8):
                    b_ = wst[cnt % 2]
                    C.load(b_.t[:], w_out[kc * 128:(kc + 1) * 128, :], b_)
                    C.cp("dve" if cnt % 2 else "pool", wo.t[:, kc, :], b_.t[:], [b_], [wo])
                    cnt += 1
                fg = C.sb(st, [128, D], F32, dma=True, name="fg")
                C.load(fg.t[:], fin_g[0:1, :].partition_broadcast(128), fg)
                yT = [C.sb(st, [128, 3, 4, 128], BF16, dma=True, name="yT") for _ in range(2)]
                sg = [C.sb(st, [128, 3072], F32, dma=True, name="sg") for _ in range(2)]
                xo = [C.sb(st, [128, D], F32, dma=True, name="xo2") for _ in range(2)]
                m1 = C.sb(st, [128, 512], F32, name="m1")
                m2 = C.sb(st, [128, 512], F32, name="m2")
                mgb = C.sb(st, [128, D], BF16, name="mgb")
                mgT = C.sb(st, [128, 8, 128], BF16, name="mgT")
                xnew = [C.sb(st, [128, D], F32, dma=True, name="xnew") for _ in range(2)]
                sqj = C.sb(st, [128, D], BF16, name="sqj")
                ssf = C.sb(st, [128, 1], F32, name="ssf")
                pb = [C.ps(st, [128, 512], F32, name="pb") for _ in range(3)]
                ptr = C.ps(st, [128, 1024], BF16, name="ptr")
                po = [C.ps(st, [128, 512], F32, name="po") for _ in range(2)]
                for i in range(NSLOT):
                    par = i % 2
                    col = slice(i * 128, (i + 1) * 128)
                    rows = slice(i * 128, (i + 1) * 128)
                    y_, g_, x_, xn_ = yT[par], sg[par], xo[par], xnew[par]
                    C.load(y_.t[:], y_scr[:, :, :, col], y_, [sres("y", i)])
                    C.load(g_.t[:], sg_scr[rows, :], g_, [sres("sg", i)])
                    C.load(x_.t[:], xo_scr[rows, :], x_, [sres("xo", i // 2)])
                    for half in range(2):
                        hs = slice(half * 512, (half + 1) * 512)
                        for br in range(3):
                            for kc in range(4):
                                C.mm(pb[br].t[:], y_.t[:, br, kc, :], wb.t[:, br, kc, hs], kc == 0, kc == 3,
                                     [y_, wb], [pb[br]])
                        C.tt("dve", m1.t[:], pb[0].t[:], g_.t[:, half * 512:half * 512 + 512], ALU.mult, [pb[0], g_], [m1])
                        C.tt("dve", m2.t[:], pb[1].t[:], g_.t[:, 1024 + half * 512:1024 + half * 512 + 512], ALU.mult,
                             [pb[1], g_], [m2])
                        C.tt("pool", m1.t[:], m1.t[:], m2.t[:], ALU.add, [m1, m2], [m1])
                        C.tt("dve", m2.t[:], pb[2].t[:], g_.t[:, 2048 + half * 512:2048 + half * 512 + 512], ALU.mult,
                             [pb[2], g_], [m2])
                        C.tt("pool", mgb.t[:, hs], m1.t[:], m2.t[:], ALU.add, [m1, m2], [mgb])
                    for kc in range(8):
                        C.tr(ptr.t[:, kc * 128:(kc + 1) * 128], mgb.t[:, kc * 128:(kc + 1) * 128], ident.t[:],
                             [mgb, ident], [ptr])
                    C.cp("act", mgT.t[:], ptr.t[:].rearrange("p (a t) -> p a t", a=8), [ptr], [mgT])
                    for half in range(2):
                        hs = slice(half * 512, (half + 1) * 512)
                        for kc in range(8):
                            C.mm(po[half].t[:], mgT.t[:, kc, :], wo.t[:, kc, hs], kc == 0, kc == 7, [mgT, wo], [po[half]])
                        C.tt("dve", xn_.t[:, hs], po[half].t[:], x_.t[:, hs], ALU.add, [po[half], x_], [xn_])
                    if final:
                        C.act(sqj.t[:], xn_.t[:], AF.Square, [xn_], [sqj, ssf], accum_out=ssf.t[:])
                        C.act(ssf.t[:], ssf.t[:], AF.Sqrt, [ssf], [ssf], scale=1.0 / D, bias=EPS)
                        C.recip(ssf.t[:], ssf.t[:], [ssf], [ssf])
                        C.stt(xn_.t[:], xn_.t[:], ssf.t[:, 0:1], fg.t[:], ALU.mult, ALU.mult, [xn_, ssf, fg], [xn_])
                    C.store(dst_rows(i), xn_.t[:], xn_, [sres("out", i)])
                S.barrier()
                S.emit()
                C.end_phase()

        emit_pass(xfull, runK=True, final=False, dst_rows=lambda i: x1full[(2 * i) * 128:(2 * i + 1) * 128, :],
                  **LW[0], **TAB["0"])
        emit_pass(xfull, runK=False, final=False, dst_rows=lambda i: x1full[(2 * i + 1) * 128:(2 * i + 2) * 128, :],
                  **LW[0], **TAB["1"])
        emit_pass(x1full, runK=True, final=True, dst_rows=lambda i: out_d[i * 128:(i + 1) * 128, :],
                  **LW[1], **TAB["o"])
        S.barrier()
        S.emit()
    return nc


_PROG = {}


def _get_prog():
    if "p" not in _PROG:
        _PROG["p"] = build_program()
    return _PROG["p"]


def _maps(inp):
    consts = [_tables(0), _tables(1)]
    maps = []
    for c in range(8):
        b, j = c // 2, c % 2
        m = {
            "xfull": np.ascontiguousarray(inp["x"][b]),
            "mem": np.ascontiguousarray(inp["mem"][b]),
            "relb": np.ascontiguousarray(inp["rel_bias"].reshape(1, 256)),
            "fin_g": np.ascontiguousarray(inp["final_g"].reshape(1, D)),
            "ident": np.eye(128).astype(ml_dtypes.bfloat16),
        }
        for l in range(2):
            m["w_in%d" % l] = np.ascontiguousarray(inp["w_in"][l])
            m["norm_g%d" % l] = np.ascontiguousarray(inp["norm_g"][l].reshape(8, 128).T)
            m["pool_w%d" % l] = np.ascontiguousarray(inp["pool_w"][l])
            m["pool_scale%d" % l] = np.ascontiguousarray(inp["pool_scale"][l].reshape(4, 128).T)
            m["mem_g%d" % l] = np.ascontiguousarray(inp["mem_norm_g"][l].reshape(8, 128).T)
            m["w_mkv%d" % l] = np.ascontiguousarray(inp["w_mem_kv"][l])
            m["w_br%d" % l] = np.ascontiguousarray(inp["w_branch"][l])
            m["w_out%d" % l] = np.ascontiguousarray(inp["w_out"][l])
        for tb, p in (("0", 0), ("1", 1), ("o", j)):
            ind, pen, invc = consts[p]
            blend = np.zeros((128, 2), np.float32)
            blend[:, p] = 1.0
            m["ind" + tb], m["pen" + tb], m["invc" + tb], m["blend" + tb] = ind, pen, invc, blend
        maps.append(m)
    return maps


def _assemble(results):
    full = np.empty((4, NB, 128, D), np.float32)
    for b in range(4):
        for j in range(2):
            full[b, j::2] = results[2 * b + j]["out"].reshape(NSLOT, 128, D)
    return full.reshape(4, NT, D)


def kernel(x, mem, norm_g, w_in, pool_w, pool_scale, mem_norm_g, w_mem_kv, w_branch, w_out, rel_bias, final_g):
    inp = {k: np.asarray(v, dtype=np.float32) for k, v in dict(
        x=x, mem=mem, norm_g=norm_g, w_in=w_in, pool_w=pool_w, pool_scale=pool_scale, mem_norm_g=mem_norm_g,
        w_mem_kv=w_mem_kv, w_branch=w_branch, w_out=w_out, rel_bias=rel_bias, final_g=final_g).items()}
    nc = _get_prog()
    res = run_bass_kernel_spmd(nc, _maps(inp), core_ids=list(range(8)))
    return _assemble(res.results)
```

```python
import itertools
import math
import os
import numpy as np
import ml_dtypes
from contextlib import ExitStack
import concourse.bass as bass
import concourse.mybir as mybir
from concourse.bass_utils import run_bass_kernel_spmd

F32 = mybir.dt.float32
BF16 = mybir.dt.bfloat16
AF = mybir.ActivationFunctionType
ALU = mybir.AluOpType
AX = mybir.AxisListType

D = 1024
NT = 8192
NB = 64
NSLOT = 32
NOWN = 4096
INC = 7752
EPS = 1e-6
NEG = -1e30
ATTN_SCALE = 64 ** -0.5
MEM_SCALE = 128 ** -0.5
IDX_W_SCALE = (64 ** -0.5) * (8 ** -0.5)
NBIS = 12
TOPK = 256
C_PU, C_PZ, C_AQ, C_AK, C_AV, C_AZ, C_IQ, C_IK, C_IW, C_MQ, C_MZ, C_G = (
    0, 512, 1024, 1536, 2048, 2560, 3072, 3584, 3648, 3656, 4168, 4680)
NNEAR = 7


class Res:
    __slots__ = ("name", "w", "r", "excl")

    def __init__(self, name="", excl=False):
        self.name = name
        self.w = None
        self.r = {}
        self.excl = excl


class DmaSem:
    __slots__ = ("key", "sem", "count")

    def __init__(self, key, sem):
        self.key = key
        self.sem = sem
        self.count = 0


class Sched:
    ENGS = ("pe", "act", "dve", "pool", "sp")

    def __init__(self, nc, stack):
        self.nc = nc
        self.stack = stack
        self.sems = {}
        self.tick = {}
        self.waited = {}
        self.ops = {}
        self.dmasems = []
        self.free = []
        self.epoch = -1
        self.nins = 0
        for e in self.ENGS:
            self.ops[e] = []
        self._new_epoch()

    def _ekey(self, e):
        return "E_%s_%d" % (e, self.epoch)

    def _new_epoch(self):
        self.epoch += 1
        for e in self.ENGS:
            phys = "P_%s_%d" % (e, self.epoch % 3)
            if phys not in self.sems:
                self.sems[phys] = self.stack.enter_context(self.nc.semaphore("sem_%s_%d" % (e, self.epoch % 3)))
            self.sems[self._ekey(e)] = self.sems[phys]
            self.tick[e] = 0
            self.waited[e] = {}
            if self.epoch >= 2:
                nxt = "P_%s_%d" % (e, (self.epoch + 1) % 3)
                self.ops[e].append(((), ("clear", nxt), None))

    def dma_sem(self):
        if self.free:
            return self.free.pop()
        key = "D_%d" % len(self.dmasems)
        sem = self.stack.enter_context(self.nc.semaphore("dsem_%d" % len(self.dmasems)))
        self.sems[key] = sem
        d = DmaSem(key, sem)
        self.dmasems.append(d)
        return d

    def _need(self, eng, waits, ev, same_ok):
        if ev is None:
            return
        key, val, src = ev
        if key.startswith("E_") and not key.endswith("_%d" % self.epoch):
            return
        if src == eng and not same_ok:
            return
        if self.waited[eng].get(key, 0) >= val:
            return
        if waits.get(key, 0) < val:
            waits[key] = val

    def op(self, eng, fn, reads=(), writes=(), dma=None):
        waits = {}
        raw_same = (eng != "pe")
        for R in reads:
            self._need(eng, waits, R.w, raw_same)
            if R.excl:
                for key, (val, src) in R.r.items():
                    self._need(eng, waits, (key, val, src), False)
        for R in writes:
            if R.w is not None and not (dma is not None and R.w[0] == dma.key):
                self._need(eng, waits, R.w, False)
            for key, (val, src) in R.r.items():
                self._need(eng, waits, (key, val, src), False)
        for key, val in waits.items():
            self.waited[eng][key] = val
        if dma is None:
            self.tick[eng] += 1
            ev = (self._ekey(eng), self.tick[eng], eng)
            inc = (self._ekey(eng), 1)
        else:
            dma.count += 16
            ev = (dma.key, dma.count, "dma")
            inc = (dma.key, 16)
        self.ops[eng].append((tuple(waits.items()), fn, inc))
        self.nins += 1
        for R in reads:
            old = R.r.get(ev[0])
            if old is None or old[0] < ev[1]:
                R.r[ev[0]] = (ev[1], ev[2])
        for R in writes:
            R.w = ev
            R.r = {}
        return ev

    def barrier(self):
        for eng in self.ENGS:
            waits = {}
            for e2 in self.ENGS:
                if e2 != eng and self.tick[e2] > 0:
                    self._need(eng, waits, (self._ekey(e2), self.tick[e2], e2), True)
            for d in self.dmasems:
                if d.count > 0:
                    self._need(eng, waits, (d.key, d.count, "dma"), True)
            self.ops[eng].append((tuple(waits.items()), None, None))
        self._new_epoch()
        for eng in self.ENGS:
            for d in self.dmasems:
                if d.count > 0:
                    self.waited[eng][d.key] = d.count

    def emit(self):
        nc = self.nc
        sems = self.sems

        def run(handle, ops):
            for waits, fn, inc in ops:
                for key, val in waits:
                    handle.wait_ge(sems[key], val)
                if isinstance(fn, tuple):
                    handle.sem_clear(sems[fn[1]])
                elif fn is not None:
                    ins = fn(handle)
                    ins.then_inc(sems[inc[0]], inc[1])

        ops = self.ops
        with nc.Block() as block:
            @block.sync
            def _(h):
                run(h, ops["sp"])

            @block.tensor
            def _(h):
                run(h, ops["pe"])

            @block.scalar
            def _(h):
                run(h, ops["act"])

            @block.vector
            def _(h):
                run(h, ops["dve"])

            @block.gpsimd
            def _(h):
                run(h, ops["pool"])
        self.ops = {e: [] for e in self.ENGS}


class Buf:
    __slots__ = ("t", "res", "sem")

    def __init__(self, t, res, sem=None):
        self.t = t
        self.res = res
        self.sem = sem


class Ctx:
    def __init__(self, nc, S):
        self.nc = nc
        self.S = S
        self.uid = 0
        self.rr = 0
        self.recycle = False
        self.phase_sems = []

    def sb(self, st, shape, dt, dma=False, name=None):
        self.uid += 1
        t = st.enter_context(self.nc.sbuf_tensor("%s_%d" % (name or "sb", self.uid), list(shape), dt))
        sem = self.S.dma_sem() if dma else None
        if sem is not None and self.recycle:
            self.phase_sems.append(sem)
        return Buf(t, Res(name or "sb"), sem)

    def end_phase(self):
        self.S.free.extend(self.phase_sems)
        self.phase_sems = []

    def ps(self, st, shape, dt, name=None):
        self.uid += 1
        t = st.enter_context(self.nc.psum_tensor("%s_%d" % (name or "ps", self.uid), list(shape), dt))
        return Buf(t, Res(name or "ps", excl=True))

    def mm(self, out, lhsT, rhs, start, stop, reads, writes, nocheck=False):
        kw = {"skip_group_check": True} if nocheck else {}
        self.S.op("pe", lambda e: e.matmul(out=out, lhsT=lhsT, rhs=rhs, start=start, stop=stop, **kw),
                  [b.res for b in reads], [b.res for b in writes])

    def tr(self, out, in_, ident, reads, writes):
        self.S.op("pe", lambda e: e.transpose(out=out, in_=in_, identity=ident),
                  [b.res for b in reads], [b.res for b in writes])

    def act(self, out, in_, func, reads, writes, scale=None, bias=None, accum_out=None):
        kw = {}
        if scale is not None:
            kw["scale"] = scale
        if bias is not None:
            kw["bias"] = bias
        if accum_out is not None:
            kw["accum_out"] = accum_out
        self.S.op("act", lambda e: e.activation(out=out, in_=in_, func=func, **kw),
                  [b.res for b in reads], [b.res for b in writes])

    def ts(self, eng, out, in0, s1, s2, op0, op1, reads, writes, accum_out=None):
        kw = {}
        if op1 is not None:
            kw["op1"] = op1
        if accum_out is not None:
            kw["accum_out"] = accum_out
        self.S.op(eng, lambda e: e.tensor_scalar(out=out, in0=in0, scalar1=s1, scalar2=s2, op0=op0, **kw),
                  [b.res for b in reads], [b.res for b in writes])

    def tt(self, eng, out, in0, in1, op, reads, writes):
        self.S.op(eng, lambda e: e.tensor_tensor(out=out, in0=in0, in1=in1, op=op),
                  [b.res for b in reads], [b.res for b in writes])

    def stt(self, out, in0, scalar, in1, op0, op1, reads, writes):
        self.S.op("dve", lambda e: e.scalar_tensor_tensor(out=out, in0=in0, scalar=scalar, in1=in1, op0=op0, op1=op1),
                  [b.res for b in reads], [b.res for b in writes])

    def cp(self, eng, out, in_, reads, writes):
        if eng == "act":
            self.S.op("act", lambda e: e.copy(out=out, in_=in_), [b.res for b in reads], [b.res for b in writes])
        else:
            self.S.op(eng, lambda e: e.tensor_copy(out=out, in_=in_), [b.res for b in reads], [b.res for b in writes])

    def cp_rr(self, out, in_, reads, writes):
        self.rr += 1
        self.cp("act" if self.rr % 2 else "dve", out, in_, reads, writes)

    def memset(self, eng, out, val, writes):
        self.S.op(eng, lambda e: e.memset(out, val), [], [b.res for b in writes])

    def recip(self, out, in_, reads, writes):
        self.S.op("dve", lambda e: e.reciprocal(out=out, in_=in_), [b.res for b in reads], [b.res for b in writes])

    def reduce(self, out, in_, op, reads, writes, absv=False):
        kw = {"apply_absolute_value": True} if absv else {}
        self.S.op("dve", lambda e: e.tensor_reduce(out=out, in_=in_, axis=AX.X, op=op, **kw),
                  [b.res for b in reads], [b.res for b in writes])

    def load(self, out, in_, dst, src_res=()):
        self.S.op("sp", lambda e: e.dma_start(out=out, in_=in_), list(src_res), [dst.res], dma=dst.sem)

    def store(self, out, in_, src, dst_res):
        self.S.op("pool", lambda e: e.dma_start(out=out, in_=in_), [src.res], list(dst_res), dma=src.sem)


def _rel_bucket_np(rel):
    half = 16
    max_exact = 8
    ret = np.where(rel > 0, half, 0)
    n = np.abs(rel)
    nf = np.maximum(n, 1).astype(np.float32)
    large = max_exact + (np.log(nf / np.float32(max_exact)) / np.float32(math.log(1024 / max_exact))
                         * np.float32(half - max_exact)).astype(np.int32)
    large = np.minimum(large, half - 1)
    return ret + np.where(n < max_exact, n, large)


def _tile_info(d):
    s = np.arange(128)[:, None] + 128 * d
    t = np.arange(128)[None, :]
    rel = (s - t).astype(np.int32)
    limit = (t // 64 + 1) * 64
    adm = s < limit
    return _rel_bucket_np(rel), adm


_PAIRS = None


def _pairs():
    global _PAIRS
    if _PAIRS is None:
        pairs = []
        for n in range(NNEAR):
            bs = set()
            for j in (0, 1):
                d = 1 - n - j
                bk, adm = _tile_info(d)
                bs |= set(np.unique(bk[adm]).tolist())
            for b in sorted(bs):
                pairs.append((n, int(b)))
        _PAIRS = pairs
    return _PAIRS


def _tables(j):
    pairs = _pairs()
    ind = np.zeros((128, len(pairs), 128), np.float32)
    for pi, (n, b) in enumerate(pairs):
        d = 1 - n - j
        bk, adm = _tile_info(d)
        ind[:, pi, :] = ((bk == b) & adm)
    pen = np.zeros((128, 256), np.float32)
    for c, n in ((0, 1), (1, 0)):
        d = 1 - n - j
        _, adm = _tile_info(d)
        pen[:, c * 128:(c + 1) * 128] = np.where(adm.T, 0.0, NEG)
    invc = np.zeros((128, 4, 128), np.float32)
    tpos = np.arange(128) + 128 * j
    for g, w in enumerate((2, 4, 8, 16)):
        invc[:, g, :] = (1.0 / np.minimum(w, tpos + 1))[None, :]
    return ind.astype(ml_dtypes.bfloat16), pen, invc


def build_program(debug=False, phases=('pro', 'K', 'P', 'A1', 'A2')):
    nc = bass.Bass("TRN2", target_bir_lowering=False)
    npairs = len(_pairs())

    def din(name, shape, dt=F32):
        return nc.dram_tensor(name, list(shape), dt, kind="ExternalInput").ap()

    def dscr(name, shape, dt):
        return nc.dram_tensor(name, list(shape), dt, kind="ExternalOutput" if (debug and name in debug) else "Internal").ap()

    xfull = din("xfull", [NT, D])
    LW = []
    for l in range(2):
        LW.append(dict(
            w_in=din("w_in%d" % l, [D, INC]), norm_g=din("norm_g%d" % l, [128, 8]),
            pool_w=din("pool_w%d" % l, [4, 128, 128]), pool_scale=din("pool_scale%d" % l, [128, 4]),
            mem_g=din("mem_g%d" % l, [128, 8]), w_mkv=din("w_mkv%d" % l, [D, 1024]),
            w_br=din("w_br%d" % l, [3, 512, D]), w_out=din("w_out%d" % l, [D, D])))
    mem = din("mem", [256, D])
    relb = din("relb", [1, 256])
    fin_g = din("fin_g", [1, D])
    ident_d = din("ident", [128, 128], BF16)
    TAB = {}
    for tb in ("0", "1", "o"):
        TAB[tb] = dict(ind_d=din("ind" + tb, [128, npairs, 128], BF16), pen_d=din("pen" + tb, [128, 256]),
                       invc_d=din("invc" + tb, [128, 4, 128]), blend_d=din("blend" + tb, [128, 2]))
    x1full = nc.dram_tensor("x1full", [NT, D], F32, kind="Internal").ap()
    out_d = nc.dram_tensor("out", [NOWN, D], F32, kind="ExternalOutput").ap()

    k_scr = dscr("k_scr", [128, 4, NT], BF16)
    ik_scr = dscr("ik_scr", [128, NT], BF16)
    v_scr = dscr("v_scr", [NT, 8 * 65], BF16)
    u_scr = dscr("u_scr", [128, 4, 16 + NT], F32)
    q_scr = dscr("q_scr", [128, 4, NOWN], BF16)
    iq_scr = dscr("iq_scr", [128, 4, NOWN], BF16)
    mq_scr = dscr("mq_scr", [128, 4, NOWN], BF16)
    sz_scr = dscr("sz_scr", [128, 4, NOWN], F32)
    saz_scr = dscr("saz_scr", [NOWN, 512], F32)
    smz_scr = dscr("smz_scr", [NOWN, 512], F32)
    sg_scr = dscr("sg_scr", [NOWN, 3072], F32)
    iw_scr = dscr("iw_scr", [NOWN, 8], F32)
    xo_scr = dscr("xo_scr", [NOWN, D], F32)
    y_scr = dscr("y_scr", [128, 3, 4, NOWN], BF16)

    with ExitStack() as top:
        S = Sched(nc, top)
        C = Ctx(nc, S)
        scr_res = {}

        def sres(name, idx):
            k = (name, idx)
            if k not in scr_res:
                scr_res[k] = Res(name)
            return scr_res[k]

        ident = C.sb(top, [128, 128], BF16, dma=True, name="ident")
        EM = C.sb(top, [128, NNEAR, 8, 128], BF16, name="EM")
        mkT = C.sb(top, [128, 4, 256], BF16, name="mkT")
        mv = C.sb(top, [128, 2, 4, 129], BF16, name="mv")
        poolW = C.sb(top, [128, 4, 128], BF16, name="poolW")
        pscale = C.sb(top, [128, 4], F32, dma=True, name="pscale")
        blend = C.sb(top, [128, 2], F32, dma=True, name="blend")
        invc0 = C.sb(top, [128, 4, 128], F32, dma=True, name="invc0")
        invcc = C.sb(top, [128, 4, 128], F32, name="invcc")
        pen = C.sb(top, [128, 256], F32, dma=True, name="pen")
        ng = C.sb(top, [128, 8], F32, dma=True, name="ng")
        mg_ = C.sb(top, [128, 8], F32, dma=True, name="mg")
        halfpow = C.sb(top, [128, NBIS], F32, name="halfpow")

        C.load(ident.t[:], ident_d[:, :], ident)
        C.recycle = True

        def emit_pass(xsrc, w_in, norm_g, pool_w, pool_scale, mem_g, w_mkv, w_br, w_out,
                      ind_d, pen_d, invc_d, blend_d, runK, final, dst_rows):
            C.load(pscale.t[:], pool_scale[:, :], pscale)
            C.load(blend.t[:], blend_d[:, :], blend)
            C.load(invc0.t[:], invc_d[:, :, :], invc0)
            C.load(pen.t[:], pen_d[:, :], pen)
            C.load(ng.t[:], norm_g[:, :], ng)
            C.load(mg_.t[:], mem_g[:, :], mg_)
            for g, w in enumerate((2, 4, 8, 16)):
                C.memset("dve", invcc.t[:, g, :], 1.0 / w, [invcc])
            for k in range(NBIS):
                C.memset("dve", halfpow.t[:, k:k + 1], 2.0 ** (-k), [halfpow])

            with ExitStack() as st:
              if 'pro' in phases:
                indt = C.sb(st, [128, npairs, 128], BF16, dma=True, name="indt")
                rb = C.sb(st, [128, 32, 8], F32, dma=True, name="rb")
                eb = C.sb(st, [128, 32, 8], F32, name="eb")
                pwst = C.sb(st, [128, 4, 128], F32, dma=True, name="pwst")
                C.load(indt.t[:], ind_d[:, :, :], indt)
                C.load(rb.t[:].rearrange("p b h -> p (b h)"), relb[0:1, :].partition_broadcast(128), rb)
                C.load(pwst.t[:], pool_w.rearrange("g c d -> c g d"), pwst)
                C.cp("dve", poolW.t[:], pwst.t[:], [pwst], [poolW])
                C.tt("dve", eb.t[:], rb.t[:], rb.t[:, 15:16, :].broadcast_to([128, 32, 8]), ALU.subtract, [rb], [eb])
                C.act(eb.t[:], eb.t[:], AF.Exp, [eb], [eb])
                C.memset("dve", EM.t[:], 0.0, [EM])
                for pi, (n, b) in enumerate(_pairs()):
                    for h in range(8):
                        C.stt(EM.t[:, n, h, :], indt.t[:, pi, :], eb.t[:, b, h:h + 1], EM.t[:, n, h, :],
                              ALU.mult, ALU.add, [indt, eb, EM], [EM])

                wm = C.sb(st, [128, 8, 1024], BF16, name="wm")
                wst = [C.sb(st, [128, 1024], F32, dma=True, name="wst") for _ in range(2)]
                for kc in range(8):
                    b_ = wst[kc % 2]
                    C.load(b_.t[:], w_mkv[kc * 128:(kc + 1) * 128, :], b_)
                    C.ts("dve" if kc % 2 else "pool", wm.t[:, kc, :], b_.t[:], mg_.t[:, kc:kc + 1], None, ALU.mult, None,
                         [b_, mg_], [wm])
                mt_ = C.sb(st, [128, 2, 1024], F32, dma=True, name="memt")
                C.load(mt_.t[:], mem.rearrange("(a p) d -> p a d", p=128), mt_)
                sqj = C.sb(st, [128, 1024], BF16, name="sqj")
                ss = C.sb(st, [128, 2], F32, name="ss")
                memn = C.sb(st, [128, 2, 1024], BF16, name="memn")
                memnT = C.sb(st, [128, 8, 256], BF16, name="memnT")
                for a in range(2):
                    C.act(sqj.t[:], mt_.t[:, a, :], AF.Square, [mt_], [sqj, ss], accum_out=ss.t[:, a:a + 1])
                C.act(ss.t[:], ss.t[:], AF.Sqrt, [ss], [ss], scale=1.0 / D, bias=EPS)
                C.recip(ss.t[:], ss.t[:], [ss], [ss])
                for a in range(2):
                    C.act(memn.t[:, a, :], mt_.t[:, a, :], AF.Copy, [mt_, ss], [memn], scale=ss.t[:, a:a + 1])
                ptr = C.ps(st, [128, 1024], BF16, name="ptr")
                pmm = [C.ps(st, [128, 512], F32, name="pmm") for _ in range(2)]
                for kc in range(8):
                    for a in range(2):
                        C.tr(ptr.t[:, a * 128:(a + 1) * 128], memn.t[:, a, kc * 128:(kc + 1) * 128], ident.t[:],
                             [memn, ident], [ptr])
                    C.cp_rr(memnT.t[:, kc, :], ptr.t[:, 0:256], [ptr], [memnT])
                for h in range(4):
                    p_ = pmm[h % 2]
                    for kc in range(8):
                        C.mm(p_.t[:, 0:256], wm.t[:, kc, h * 128:(h + 1) * 128], memnT.t[:, kc, :], kc == 0, kc == 7,
                             [wm, memnT], [p_])
                    C.cp_rr(mkT.t[:, h, :], p_.t[:, 0:256], [p_], [mkT])
                C.memset("dve", mv.t[:], 1.0, [mv])
                for a in range(2):
                    p_ = pmm[a % 2]
                    for kc in range(8):
                        C.mm(p_.t[:], memnT.t[:, kc, a * 128:(a + 1) * 128], wm.t[:, kc, 512:1024], kc == 0, kc == 7,
                             [wm, memnT], [p_])
                    C.cp_rr(mv.t[:, a, :, 0:128], p_.t[:].rearrange("p (h d) -> p h d", h=4), [p_], [mv])
                S.barrier()
                S.emit()
                C.end_phase()

            with ExitStack() as st:
              if 'K' in phases and runK:
                NK = 1664
                wk = C.sb(st, [128, 8, NK], BF16, name="wk")
                wst = [C.sb(st, [128, NK], F32, dma=True, name="wstk") for _ in range(2)]
                for kc in range(8):
                    b_ = wst[kc % 2]
                    rows = slice(kc * 128, (kc + 1) * 128)
                    C.load(b_.t[:, 0:512], w_in[rows, C_AK:C_AK + 512], b_)
                    C.load(b_.t[:, 512:576], w_in[rows, C_IK:C_IK + 64], b_)
                    C.load(b_.t[:, 576:640], w_in[rows, C_IK:C_IK + 64], b_)
                    C.load(b_.t[:, 640:1152], w_in[rows, C_PU:C_PU + 512], b_)
                    C.load(b_.t[:, 1152:1664], w_in[rows, C_AV:C_AV + 512], b_)
                    C.ts("dve" if kc % 2 else "pool", wk.t[:, kc, :], b_.t[:], ng.t[:, kc:kc + 1], None, ALU.mult, None,
                         [b_, ng], [wk])
                zt = C.sb(st, [128, 4, 16], F32, dma=True, name="zt")
                C.memset("dve", zt.t[:], 0.0, [zt])
                C.store(u_scr[:, :, 0:16], zt.t[:], zt, [sres("u", -1)])
                xt = [C.sb(st, [128, 4, D], F32, dma=True, name="xt") for _ in range(2)]
                sqj = C.sb(st, [128, D], BF16, name="sqj")
                ss = [C.sb(st, [128, 4], F32, name="ss") for _ in range(2)]
                xn = [C.sb(st, [128, 4, D], BF16, name="xn") for _ in range(2)]
                xnT = [C.sb(st, [128, 8, 512], BF16, name="xnT") for _ in range(2)]
                kst = [C.sb(st, [128, 4, 512], BF16, dma=True, name="kst") for _ in range(2)]
                ikst = [C.sb(st, [128, 512], BF16, dma=True, name="ikst") for _ in range(2)]
                ust = [C.sb(st, [128, 4, 512], F32, dma=True, name="ust") for _ in range(2)]
                vst = [C.sb(st, [128, 4, 8, 65], BF16, dma=True, name="vst") for _ in range(2)]
                for b_ in vst:
                    C.memset("dve", b_.t[:], 1.0, [b_])
                ptr = [C.ps(st, [128, 1024], BF16, name="ptr") for _ in range(2)]
                pf = [C.ps(st, [128, 512], F32, name="pf") for _ in range(2)]
                pv = [C.ps(st, [128, 512], F32, name="pv") for _ in range(2)]
                for T in range(16):
                    x_ = xt[T % 2]
                    s_ = ss[T % 2]
                    n_ = xn[T % 2]
                    nT = xnT[T % 2]
                    C.load(x_.t[:], xsrc[T * 512:(T + 1) * 512, :].rearrange("(a p) d -> p a d", p=128), x_,
                           [sres("xfull", T)])
                    for a in range(4):
                        C.act(sqj.t[:], x_.t[:, a, :], AF.Square, [x_], [sqj, s_], accum_out=s_.t[:, a:a + 1])
                    C.act(s_.t[:], s_.t[:], AF.Sqrt, [s_], [s_], scale=1.0 / D, bias=EPS)
                    C.recip(s_.t[:], s_.t[:], [s_], [s_])
                    for a in range(4):
                        if a % 2 == 0:
                            C.act(n_.t[:, a, :], x_.t[:, a, :], AF.Copy, [x_, s_], [n_], scale=s_.t[:, a:a + 1])
                        else:
                            C.ts("dve", n_.t[:, a, :], x_.t[:, a, :], s_.t[:, a:a + 1], None, ALU.mult, None, [x_, s_], [n_])
                    for kc in range(8):
                        p_ = ptr[kc % 2]
                        for a in range(4):
                            C.tr(p_.t[:, a * 128:(a + 1) * 128], n_.t[:, a, kc * 128:(kc + 1) * 128], ident.t[:],
                                 [n_, ident], [p_])
                        C.cp_rr(nT.t[:, kc, :], p_.t[:, 0:512], [p_], [nT])
                    ks, iks, us, vs = kst[T % 2], ikst[T % 2], ust[T % 2], vst[T % 2]
                    for ct in range(9):
                        p_ = pf[ct % 2]
                        for kc in range(8):
                            C.mm(p_.t[:], wk.t[:, kc, ct * 128:(ct + 1) * 128], nT.t[:, kc, :], kc == 0, kc == 7,
                                 [wk, nT], [p_])
                        if ct < 4:
                            C.cp_rr(ks.t[:, ct, :], p_.t[:], [p_], [ks])
                        elif ct == 4:
                            C.cp_rr(iks.t[:], p_.t[:], [p_], [iks])
                        else:
                            C.cp_rr(us.t[:, ct - 5, :], p_.t[:], [p_], [us])
                    C.store(k_scr[:, :, T * 512:(T + 1) * 512], ks.t[:], ks, [sres("k", T)])
                    C.store(ik_scr[:, T * 512:(T + 1) * 512], iks.t[:], iks, [sres("ik", T)])
                    C.store(u_scr[:, :, 16 + T * 512:16 + (T + 1) * 512], us.t[:], us, [sres("u", T)])
                    for a in range(4):
                        p_ = pv[a % 2]
                        for kc in range(8):
                            C.mm(p_.t[:], nT.t[:, kc, a * 128:(a + 1) * 128], wk.t[:, kc, 1152:1664], kc == 0, kc == 7,
                                 [wk, nT], [p_])
                        C.cp_rr(vs.t[:, a, :, 0:64], p_.t[:].rearrange("p (h d) -> p h d", h=8), [p_], [vs])
                    C.store(v_scr[T * 512:(T + 1) * 512, :].rearrange("(a p) c -> p a c", p=128),
                            vs.t[:].rearrange("p a h c -> p a (h c)"), vs, [sres("v", T)])
                S.barrier()
                S.emit()
                C.end_phase()

            with ExitStack() as st:
              if 'P' in phases:
                NF = 2048
                NTM = 4104
                NP = NF + NTM
                wp = C.sb(st, [128, 8, NP], BF16, name="wp")
                tmst = [C.sb(st, [128, 2048], F32, dma=True, name="tmst") for _ in range(2)]
                cnt = 0
                for kc in range(8):
                    rows = slice(kc * 128, (kc + 1) * 128)
                    groups = [
                        (0, [(C_PZ, 512), (C_AQ, 512), (C_IQ, 512), (C_MQ, 512)]),
                        (2048, [(C_AZ, 512), (C_MZ, 512), (C_G, 1024)]),
                        (4096, [(C_G + 1024, 2048)]),
                        (6144, [(C_IW, 8)]),
                    ]
                    for (dst0, parts) in groups:
                        b_ = tmst[cnt % 2]
                        o = 0
                        for (c0, n) in parts:
                            C.load(b_.t[:, o:o + n], w_in[rows, c0:c0 + n], b_)
                            o += n
                        C.ts("dve" if cnt % 2 else "pool", wp.t[:, kc, dst0:dst0 + o], b_.t[:, 0:o], ng.t[:, kc:kc + 1],
                             None, ALU.mult, None, [b_, ng], [wp])
                        cnt += 1
                cand = [C.sb(st, [128, 4, D], F32, dma=True, name="cand") for _ in range(1)]
                xo = [C.sb(st, [128, 2, D], F32, dma=True, name="xo") for _ in range(1)]
                sqj = C.sb(st, [128, D], BF16, name="sqj")
                ss = [C.sb(st, [128, 2], F32, name="ss") for _ in range(2)]
                xn = C.sb(st, [128, 2, D], BF16, name="xn")
                xnT = [C.sb(st, [128, 8, 256], BF16, name="xnT") for _ in range(2)]
                szst = [C.sb(st, [128, 4, 256], F32, dma=True, name="szst") for _ in range(1)]
                fmst = [[C.sb(st, [128, 4, 256], BF16, dma=True, name="fmst") for _ in range(1)] for _ in range(3)]
                iwst = [C.sb(st, [128, 8], F32, dma=True, name="iwst") for _ in range(2)]
                ptr = [C.ps(st, [128, 1024], BF16, name="ptr") for _ in range(2)]
                pf = [C.ps(st, [128, 512], F32, name="pf") for _ in range(2)]
                pt = [C.ps(st, [128, 512], F32, name="pt") for _ in range(3)]
                for U in range(16):
                    c_ = cand[0]
                    x_ = xo[0]
                    s_ = ss[U % 2]
                    nT = xnT[U % 2]
                    C.load(c_.t[:], xsrc[U * 512:(U + 1) * 512, :].rearrange("(a p) d -> p a d", p=128), c_,
                           [sres("xfull", U)])
                    C.ts("dve", x_.t[:], c_.t[:, 0:4:2, :], blend.t[:, 0:1], None, ALU.mult, None, [c_, blend], [x_])
                    C.stt(x_.t[:], c_.t[:, 1:4:2, :], blend.t[:, 1:2], x_.t[:], ALU.mult, ALU.add, [c_, blend, x_], [x_])
                    C.store(xo_scr[U * 256:(U + 1) * 256, :].rearrange("(a p) d -> p a d", p=128), x_.t[:], x_,
                            [sres("xo", U)])
                    for a in range(2):
                        C.act(sqj.t[:], x_.t[:, a, :], AF.Square, [x_], [sqj, s_], accum_out=s_.t[:, a:a + 1])
                    C.act(s_.t[:], s_.t[:], AF.Sqrt, [s_], [s_], scale=1.0 / D, bias=EPS)
                    C.recip(s_.t[:], s_.t[:], [s_], [s_])
                    C.act(xn.t[:, 0, :], x_.t[:, 0, :], AF.Copy, [x_, s_], [xn], scale=s_.t[:, 0:1])
                    C.ts("dve", xn.t[:, 1, :], x_.t[:, 1, :], s_.t[:, 1:2], None, ALU.mult, None, [x_, s_], [xn])
                    for kc in range(8):
                        p_ = ptr[kc % 2]
                        for a in range(2):
                            C.tr(p_.t[:, a * 128:(a + 1) * 128], xn.t[:, a, kc * 128:(kc + 1) * 128], ident.t[:],
                                 [xn, ident], [p_])
                        C.cp_rr(nT.t[:, kc, :], p_.t[:, 0:256], [p_], [nT])
                    szs = szst[0]
                    fms = [fmst[k][0] for k in range(3)]
                    for ct in range(16):
                        p_ = pf[ct % 2]
                        for kc in range(8):
                            C.mm(p_.t[:, 0:256], wp.t[:, kc, ct * 128:(ct + 1) * 128], nT.t[:, kc, :], kc == 0, kc == 7,
                                 [wp, nT], [p_])
                        if ct < 4:
                            C.act(szs.t[:, ct, :], p_.t[:, 0:256], AF.Silu, [p_], [szs])
                        else:
                            f_ = fms[ct // 4 - 1]
                            C.cp("dve", f_.t[:, ct % 4, :], p_.t[:, 0:256], [p_], [f_])
                    col = slice(U * 256, (U + 1) * 256)
                    C.store(sz_scr[:, :, col], szs.t[:], szs, [sres("sz", U)])
                    C.store(q_scr[:, :, col], fms[0].t[:], fms[0], [sres("q", U)])
                    C.store(iq_scr[:, :, col], fms[1].t[:], fms[1], [sres("iq", U)])
                    C.store(mq_scr[:, :, col], fms[2].t[:], fms[2], [sres("mq", U)])
                    for a in range(2):
                        iws = iwst[a]
                        rows = slice(U * 256 + a * 128, U * 256 + (a + 1) * 128)
                        for grp in range(8):
                            tm = tmst[grp // 4]
                            p_ = pt[grp % 3]
                            for kc in range(8):
                                C.mm(p_.t[:], nT.t[:, kc, a * 128:(a + 1) * 128],
                                     wp.t[:, kc, NF + grp * 512:NF + (grp + 1) * 512], kc == 0, kc == 7, [wp, nT], [p_])
                            C.act(tm.t[:, (grp % 4) * 512:(grp % 4 + 1) * 512], p_.t[:], AF.Silu if grp < 2 else AF.Sigmoid,
                                  [p_], [tm])
                            if grp == 3:
                                C.store(saz_scr[rows, :], tm.t[:, 0:512], tm, [sres("saz", 2 * U + a)])
                                C.store(smz_scr[rows, :], tm.t[:, 512:1024], tm, [sres("smz", 2 * U + a)])
                                C.store(sg_scr[rows, 0:1024], tm.t[:, 1024:2048], tm, [sres("sg", 2 * U + a)])
                            if grp == 7:
                                C.store(sg_scr[rows, 1024:3072], tm.t[:, 0:2048], tm, [sres("sg", 2 * U + a)])
                        p_ = pt[2]
                        for kc in range(8):
                            C.mm(p_.t[:, 0:8], nT.t[:, kc, a * 128:(a + 1) * 128], wp.t[:, kc, NF + 4096:NF + 4104],
                                 kc == 0, kc == 7, [wp, nT], [p_])
                        C.ts("dve", iws.t[:], p_.t[:, 0:8], IDX_W_SCALE, None, ALU.mult, None, [p_], [iws])
                        C.store(iw_scr[rows, :], iws.t[:], iws, [sres("iw", 2 * U + a)])
                S.barrier()
                S.emit()
                C.end_phase()

            with ExitStack() as st:
              if 'A1' in phases:
                scores = C.sb(st, [128, NT], F32, name="scores")
                mask01 = [C.sb(st, [128, NT], BF16, name="mask01") for _ in range(2)]
                mT = [C.sb(st, [128, 4, 128], BF16, name="mT") for _ in range(2)]
                kTc = [C.sb(st, [128, 4, 512], BF16, dma=True, name="kTc") for _ in range(2)]
                Vc = [C.sb(st, [128, 4, 520], BF16, dma=True, name="Vc") for _ in range(2)]
                ikt = [C.sb(st, [128, 512], BF16, dma=True, name="ikt") for _ in range(2)]
                Rt = [[C.sb(st, [128, 512], BF16, name="Rt") for _ in range(8)] for _ in range(2)]
                Et = [C.sb(st, [128, 4, 128], BF16, name="Et") for _ in range(2)]
                Et2 = [C.sb(st, [128, 4, 128], BF16, name="Et2") for _ in range(2)]
                Pt = [C.sb(st, [128, 4, 128], BF16, name="Pt") for _ in range(2)]
                qT = [C.sb(st, [128, 4, 128], BF16, dma=True, name="qT") for _ in range(2)]
                iqT = [C.sb(st, [128, 4, 128], BF16, dma=True, name="iqT") for _ in range(2)]
                mqT = [C.sb(st, [128, 4, 128], BF16, dma=True, name="mqT") for _ in range(2)]
                iw = [C.sb(st, [128, 8], F32, dma=True, name="iw") for _ in range(2)]
                qz = [C.sb(st, [128, 8, 128], BF16, name="qz") for _ in range(2)]
                iqz = [C.sb(st, [128, 8, 128], BF16, name="iqz") for _ in range(2)]
                for b_ in qz + iqz:
                    C.memset("dve", b_.t[:], 0.0, [b_])
                Dg = [C.sb(st, [128, 8, 128], BF16, name="Dg") for _ in range(2)]
                szT = [C.sb(st, [128, 4, 128], F32, dma=True, name="szT") for _ in range(2)]
                saz = [C.sb(st, [128, 512], F32, dma=True, name="saz") for _ in range(2)]
                smz = [C.sb(st, [128, 512], F32, dma=True, name="smz") for _ in range(2)]
                ua = [C.sb(st, [128, 4, 144], F32, dma=True, name="ua") for _ in range(2)]
                ub = [C.sb(st, [128, 4, 144], F32, dma=True, name="ub") for _ in range(2)]
                uw = C.sb(st, [128, 4, 144], F32, name="uw")
                s1 = C.sb(st, [128, 4, 144], F32, name="s1")
                s2 = C.sb(st, [128, 4, 144], F32, name="s2")
                s3 = C.sb(st, [128, 4, 144], F32, name="s3")
                Sall = C.sb(st, [128, 4, 128], F32, name="Sall")
                plb = C.sb(st, [128, 4, 128], BF16, name="plb")
                ypf = C.sb(st, [128, 4, 128], F32, name="ypf")
                amax = C.sb(st, [128, 20], F32, name="amax")
                A_ = C.sb(st, [128, 1], F32, name="A")
                steps = C.sb(st, [128, NBIS], F32, name="steps")
                lo = C.sb(st, [128, 1], F32, name="lo")
                cc = C.sb(st, [128, 1], F32, name="cc")
                cnt_ = C.sb(st, [128, 1], F32, name="cnt")
                dd = C.sb(st, [128, 1], F32, name="dd")
                rs = C.sb(st, [128, 8], F32, name="rs")
                accs = [C.sb(st, [128, 4, 65], F32, name="accs") for _ in range(2)]
                yaf = C.sb(st, [128, 8, 64], F32, name="yaf")
                yab = C.sb(st, [128, 512], BF16, name="yab")
                rsm = C.sb(st, [128, 4], F32, name="rsm")
                ymf = C.sb(st, [128, 4, 128], F32, name="ymf")
                ymb = C.sb(st, [128, 512], BF16, name="ymb")
                Pm = C.sb(st, [128, 8, 128], BF16, name="Pm")
                yst = [C.sb(st, [128, 3, 4, 128], BF16, dma=True, name="yst") for _ in range(2)]
                psc = [C.ps(st, [128, 512], F32, name="psc") for _ in range(2)]
                pidx = C.ps(st, [128, 512], F32, name="pidx")
                pmT = C.ps(st, [128, 1024], BF16, name="pmT")
                pl = [C.ps(st, [128, 4, 128], F32, name="pl") for _ in range(2)]
                pacc = [C.ps(st, [128, 512], F32, name="pacc") for _ in range(2)]

                NS = int(os.environ.get('A1_SLOTS', NSLOT))
                MASKENG = os.environ.get('MASKENG', 'dve')

                def idx_tiles(i):
                    L = (2 * i + 2) * 128
                    tiles = []
                    k0 = 0
                    if (L // 256) % 2 == 1:
                        tiles.append((0, 256))
                        k0 = 256
                    while k0 < L:
                        tiles.append((k0, 512))
                        k0 += 512
                    return tiles

                def prep(i):
                    par = i % 2
                    col = slice(i * 128, (i + 1) * 128)
                    rows = slice(i * 128, (i + 1) * 128)
                    U = i // 2
                    q_, iq_, mq_, iw_, Dg_ = qT[par], iqT[par], mqT[par], iw[par], Dg[par]
                    sz_, saz_, smz_, ua_, ub_ = szT[par], saz[par], smz[par], ua[par], ub[par]
                    C.load(q_.t[:], q_scr[:, :, col], q_, [sres("q", U)])
                    C.load(iq_.t[:], iq_scr[:, :, col], iq_, [sres("iq", U)])
                    C.load(mq_.t[:], mq_scr[:, :, col], mq_, [sres("mq", U)])
                    C.load(iw_.t[:], iw_scr[rows, :], iw_, [sres("iw", i)])
                    C.load(sz_.t[:], sz_scr[:, :, col], sz_, [sres("sz", U)])
                    C.load(saz_.t[:], saz_scr[rows, :], saz_, [sres("saz", i)])
                    C.load(smz_.t[:], smz_scr[rows, :], smz_, [sres("smz", i)])
                    C.load(ua_.t[:], u_scr[:, :, 256 * i:256 * i + 144], ua_, [sres("u", -1)])
                    C.load(ub_.t[:], u_scr[:, :, 256 * i + 128:256 * i + 272], ub_, [sres("u", -1)])
                    qz_, iqz_ = qz[par], iqz[par]
                    for r in range(2):
                        ps_ = slice(r * 64, (r + 1) * 64)
                        C.cp("dve", qz_.t[ps_, r:8:2, :], q_.t[ps_, :, :], [q_], [qz_])
                        C.cp("dve", iqz_.t[ps_, r:8:2, :], iq_.t[ps_, :, :], [iq_], [iqz_])
                    C.tt("dve", Dg_.t[:], ident.t[:].unsqueeze(1).broadcast_to([128, 8, 128]),
                         iw_.t[:].unsqueeze(2).broadcast_to([128, 8, 128]), ALU.mult, [ident, iw_], [Dg_])

                def gen_idx(i):
                    par = i % 2
                    Dg_, iqz_ = Dg[par], iqz[par]
                    tiles = idx_tiles(i)
                    for ti, (k0, kw) in enumerate(tiles):
                        ik_ = ikt[ti % 2]
                        R_ = Rt[ti % 2]
                        C.load(ik_.t[:, 0:kw], ik_scr[:, k0:k0 + kw], ik_, [sres("ik", k0 // 512)])

                        def sc(h):
                            p_ = psc[h % 2]
                            C.mm(p_.t[:, 0:kw], iqz_.t[:, h, :], ik_.t[:, 0:kw], True, True, [iqz_, ik_], [p_])

                        def relu(h):
                            p_ = psc[h % 2]
                            C.act(R_[h].t[:, 0:kw], p_.t[:, 0:kw], AF.Relu, [p_], [R_[h]])

                        def red(h):
                            C.mm(pidx.t[:, 0:kw], Dg_.t[:, h, :], R_[h].t[:, 0:kw], h == 0, h == 7, [Dg_, R_[h]], [pidx])
                        sc(0)
                        sc(1)
                        for h in range(8):
                            relu(h)
                            if h + 2 < 8:
                                sc(h + 2)
                            red(h)
                            yield
                        C.reduce(amax.t[:, ti:ti + 1], pidx.t[:, 0:kw], ALU.max, [pidx], [amax], absv=True)
                        if ti == len(tiles) - 1:
                            if kw > 256:
                                C.cp("act", scores.t[:, k0:k0 + kw - 256], pidx.t[:, 0:kw - 256], [pidx], [scores])
                            C.tt("dve", scores.t[:, k0 + kw - 256:k0 + kw], pidx.t[:, kw - 256:kw], pen.t[:], ALU.add,
                                 [pidx, pen], [scores])
                        else:
                            C.cp("act", scores.t[:, k0:k0 + kw], pidx.t[:, 0:kw], [pidx], [scores])
                        yield

                def gen_bis(i):
                    L = (2 * i + 2) * 128
                    m01 = mask01[i % 2]
                    C.reduce(A_.t[:], amax.t[:, 0:len(idx_tiles(i))], ALU.max, [amax], [A_])
                    C.ts("dve", A_.t[:], A_.t[:], 1.0001, 1e-20, ALU.mult, ALU.add, [A_], [A_])
                    C.ts("dve", steps.t[:], halfpow.t[:], A_.t[:, 0:1], None, ALU.mult, None, [halfpow, A_], [steps])
                    C.ts("dve", lo.t[:], A_.t[:], -1.0, None, ALU.mult, None, [A_], [lo])
                    yield
                    for k in range(NBIS):
                        C.tt("dve", cc.t[:], lo.t[:], steps.t[:, k:k + 1], ALU.add, [lo, steps], [cc])
                        C.ts("dve", m01.t[:, 0:L], scores.t[:, 0:L], cc.t[:, 0:1], 0.0, ALU.is_ge, ALU.add,
                             [scores, cc], [m01, cnt_], accum_out=cnt_.t[:])
                        C.stt(dd.t[:], cnt_.t[:], TOPK - 0.5, steps.t[:, k:k + 1], ALU.is_ge, ALU.mult,
                              [cnt_, steps], [dd])
                        C.tt("dve", lo.t[:], lo.t[:], dd.t[:], ALU.add, [lo, dd], [lo])
                        yield
                    C.ts("dve", m01.t[:, 0:L], scores.t[:, 0:L], lo.t[:, 0:1], None, ALU.is_ge, None,
                         [scores, lo], [m01])
                    yield

                def gen_att(i):
                    par = i % 2
                    nkt = 2 * i + 2
                    col = slice(i * 128, (i + 1) * 128)
                    mq_, qz_ = mqT[par], qz[par]
                    sz_, saz_, smz_, ua_, ub_ = szT[par], saz[par], smz[par], ua[par], ub[par]
                    m01 = mask01[par]
                    steps_ = [(kt, hg) for kt in range(nkt) for hg in range(2)]

                    def chunk_setup(c):
                        kt0 = c * 4
                        nk = min(4, nkt - kt0)
                        kc_, vc_, mT_ = kTc[c % 2], Vc[c % 2], mT[c % 2]
                        C.load(kc_.t[:, :, 0:nk * 128], k_scr[:, :, kt0 * 128:(kt0 + nk) * 128], kc_,
                               [sres("k", (kt0 * 128) // 512)])
                        C.load(vc_.t[:, 0:nk, :], v_scr[kt0 * 128:(kt0 + nk) * 128, :].rearrange("(a p) c -> p a c", p=128),
                               vc_, [sres("v", (kt0 * 128) // 512)])
                        for jj in range(nk):
                            kt = kt0 + jj
                            C.tr(pmT.t[:, jj * 128:(jj + 1) * 128], m01.t[:, kt * 128:(kt + 1) * 128], ident.t[:],
                                 [m01, ident], [pmT])
                        C.cp("act", mT_.t[:, 0:nk, :], pmT.t[:, 0:nk * 128].rearrange("p (a t) -> p a t", a=nk),
                             [pmT], [mT_])

                    def stA(sidx):
                        kt, hg = steps_[sidx]
                        c, jj = kt // 4, kt % 4
                        if jj == 0 and hg == 0:
                            chunk_setup(c)
                        kc_ = kTc[c % 2]
                        p_ = pl[hg]
                        for hh in range(4):
                            h = hg * 4 + hh
                            C.mm(p_.t[:, hh, :], kc_.t[:, h // 2, jj * 128:(jj + 1) * 128],
                                 qz_.t[:, h, :], True, True, [kc_, qz_], [p_])

                    def stB(sidx):
                        kt, hg = steps_[sidx]
                        c, jj = kt // 4, kt % 4
                        n = nkt - 1 - kt
                        mT_ = mT[c % 2]
                        p_, e_, P_ = pl[hg], Et[hg], Pt[hg]
                        C.act(e_.t[:], p_.t[:], AF.Exp, [p_], [e_], scale=ATTN_SCALE)
                        mb = mT_.t[:, jj:jj + 1, :].broadcast_to([128, 4, 128])
                        if n < NNEAR:
                            e2 = Et2[hg]
                            C.tt("dve", e2.t[:], e_.t[:], EM.t[:, n, hg * 4:(hg + 1) * 4, :], ALU.mult,
                                 [e_, EM], [e2])
                            C.tt(MASKENG, P_.t[:], e2.t[:], mb, ALU.mult, [e2, mT_], [P_])
                        else:
                            C.tt(MASKENG, P_.t[:], e_.t[:], mb, ALU.mult, [e_, mT_], [P_])

                    def stC(sidx):
                        kt, hg = steps_[sidx]
                        c, jj = kt // 4, kt % 4
                        vc_ = Vc[c % 2]
                        P_, a_ = Pt[hg], pacc[hg]
                        for hh in range(4):
                            h = hg * 4 + hh
                            C.mm(a_.t[:, hh * 65:(hh + 1) * 65], P_.t[:, hh, :], vc_.t[:, jj, h * 65:(h + 1) * 65],
                                 kt == 0 and hh == 0, kt == nkt - 1, [P_, vc_], [a_], nocheck=True)

                    stA(0)
                    stA(1)
                    for sidx in range(len(steps_)):
                        stB(sidx)
                        if sidx + 2 < len(steps_):
                            stA(sidx + 2)
                        stC(sidx)
                        yield
                    ys = yst[par]
                    for hg in range(2):
                        C.cp("dve", accs[hg].t[:], pacc[hg].t[:, 0:260].rearrange("p (h c) -> p h c", h=4),
                             [pacc[hg]], [accs[hg]])
                        av = accs[hg].t[:]
                        C.recip(rs.t[:, hg * 4:(hg + 1) * 4], av[:, :, 64], [accs[hg]], [rs])
                        C.tt("dve", yaf.t[:, hg * 4:(hg + 1) * 4, :], av[:, :, 0:64],
                             rs.t[:, hg * 4:(hg + 1) * 4].unsqueeze(2).broadcast_to([128, 4, 64]), ALU.mult,
                             [accs[hg], rs], [yaf])
                    C.tt("dve", yab.t[:], yaf.t[:].rearrange("p h d -> p (h d)"), saz_.t[:], ALU.mult, [yaf, saz_], [yab])
                    for kc in range(4):
                        C.tr(pmT.t[:, kc * 128:(kc + 1) * 128], yab.t[:, kc * 128:(kc + 1) * 128], ident.t[:],
                             [yab, ident], [pmT])
                    C.cp("act", ys.t[:, 1, :, :], pmT.t[:, 0:512].rearrange("p (a t) -> p a t", a=4), [pmT], [ys])
                    yield

                    for mt in range(2):
                        p_ = pl[mt]
                        for hm in range(4):
                            C.mm(p_.t[:, hm, :], mkT.t[:, hm, mt * 128:(mt + 1) * 128], mq_.t[:, hm, :], True, True,
                                 [mkT, mq_], [p_])
                        C.act(Pm.t[:, mt * 4:(mt + 1) * 4, :], p_.t[:], AF.Exp, [p_], [Pm], scale=MEM_SCALE)
                    for hm in range(4):
                        a_ = pacc[hm // 2]
                        o = (hm % 2) * 129
                        for mt in range(2):
                            C.mm(a_.t[:, o:o + 129], Pm.t[:, mt * 4 + hm, :], mv.t[:, mt, hm, :], mt == 0, mt == 1,
                                 [Pm, mv], [a_])
                    for hp in range(2):
                        av = pacc[hp].t[:, 0:258].rearrange("p (h c) -> p h c", h=2)
                        C.recip(rsm.t[:, hp * 2:(hp + 1) * 2], av[:, :, 128], [pacc[hp]], [rsm])
                        C.tt("dve", ymf.t[:, hp * 2:(hp + 1) * 2, :], av[:, :, 0:128],
                             rsm.t[:, hp * 2:(hp + 1) * 2].unsqueeze(2).broadcast_to([128, 2, 128]), ALU.mult,
                             [pacc[hp], rsm], [ymf])
                    C.tt("dve", ymb.t[:], ymf.t[:].rearrange("p h d -> p (h d)"), smz_.t[:], ALU.mult, [ymf, smz_], [ymb])
                    for kc in range(4):
                        C.tr(pmT.t[:, kc * 128:(kc + 1) * 128], ymb.t[:, kc * 128:(kc + 1) * 128], ident.t[:],
                             [ymb, ident], [pmT])
                    C.cp("act", ys.t[:, 2, :, :], pmT.t[:, 0:512].rearrange("p (a t) -> p a t", a=4), [pmT], [ys])
                    yield

                    C.ts("dve", uw.t[:], ua_.t[:], blend.t[:, 0:1], None, ALU.mult, None, [ua_, blend], [uw])
                    C.stt(uw.t[:], ub_.t[:], blend.t[:, 1:2], uw.t[:], ALU.mult, ALU.add, [ub_, blend, uw], [uw])
                    C.tt("dve", s1.t[:, :, 1:144], uw.t[:, :, 1:144], uw.t[:, :, 0:143], ALU.add, [uw], [s1])
                    C.tt("dve", s2.t[:, 1:4, 3:144], s1.t[:, 1:4, 3:144], s1.t[:, 1:4, 1:142], ALU.add, [s1], [s2])
                    C.tt("dve", s3.t[:, 2:4, 7:144], s2.t[:, 2:4, 7:144], s2.t[:, 2:4, 3:140], ALU.add, [s2], [s3])
                    C.cp("dve", Sall.t[:, 0, :], s1.t[:, 0, 16:144], [s1], [Sall])
                    C.cp("dve", Sall.t[:, 1, :], s2.t[:, 1, 16:144], [s2], [Sall])
                    C.cp("dve", Sall.t[:, 2, :], s3.t[:, 2, 16:144], [s3], [Sall])
                    C.tt("dve", Sall.t[:, 3, :], s3.t[:, 3, 16:144], s3.t[:, 3, 8:136], ALU.add, [s3], [Sall])
                    ic = invc0 if i == 0 else invcc
                    C.tt("dve", Sall.t[:], Sall.t[:], ic.t[:], ALU.mult, [Sall, ic], [Sall])
                    C.tt("dve", plb.t[:], Sall.t[:], uw.t[:, :, 16:144], ALU.subtract, [Sall, uw], [plb])
                    pp = pl[1]
                    for g in range(4):
                        C.mm(pp.t[:, g, :], poolW.t[:, g, :], plb.t[:, g, :], True, True, [poolW, plb], [pp])
                    C.tt("dve", ypf.t[:], pp.t[:], pscale.t[:].unsqueeze(2).broadcast_to([128, 4, 128]), ALU.mult,
                         [pp, pscale], [ypf])
                    C.tt("dve", ys.t[:, 0, :, :], ypf.t[:], sz_.t[:], ALU.mult, [ypf, sz_], [ys])
                    C.store(y_scr[:, :, :, col], ys.t[:], ys, [sres("y", i)])
                    yield

                SENT = object()
                prep(0)
                for _ in gen_idx(0):
                    pass
                for _ in gen_bis(0):
                    pass
                for i in range(NS):
                    if i + 1 < NS:
                        prep(i + 1)
                        side = itertools.chain(gen_idx(i + 1), gen_bis(i + 1))
                        n_side = 9 * len(idx_tiles(i + 1)) + NBIS + 2
                    else:
                        side = iter(())
                        n_side = 0
                    n_main = 2 * (2 * i + 2) + 3
                    done = 0
                    for m, _ in enumerate(gen_att(i)):
                        target = ((m + 1) * n_side + n_main - 1) // n_main
                        while done < target:
                            if next(side, SENT) is SENT:
                                done = n_side
                                break
                            done += 1
                    for _ in side:
                        pass
                S.barrier()
                S.emit()
                C.end_phase()

            with ExitStack() as st:
              if 'A2' in phases:
                wb = C.sb(st, [128, 3, 4, D], BF16, name="wb")
                wo = C.sb(st, [128, 8, D], BF16, name="wo")
                wst = [C.sb(st, [128, D], F32, dma=True, name="wsta") for _ in range(2)]
                cnt = 0
                for br in range(3):
                    for kc in range(4):
                        b_ = wst[cnt % 2]
                        C.load(b_.t[:], w_br[br, kc * 128:(kc + 1) * 128, :], b_)
                        C.cp("dve" if cnt % 2 else "pool", wb.t[:, br, kc, :], b_.t[:], [b_], [wb])
                        cnt += 1
                for kc in range(8):
                    b_ = wst[cnt % 2]
                    C.load(b_.t[:], w_out[kc * 128:(kc + 1) * 128, :], b_)
                    C.cp("dve" if cnt % 2 else "pool", wo.t[:, kc, :], b_.t[:], [b_], [wo])
                    cnt += 1
                fg = C.sb(st, [128, D], F32, dma=True, name="fg")
                C.load(fg.t[:], fin_g[0:1, :].partition_broadcast(128), fg)
                yT = [C.sb(st, [128, 3, 4, 128], BF16, dma=True, name="yT") for _ in range(2)]
                sg = [C.sb(st, [128, 3072], F32, dma=True, name="sg") for _ in range(2)]
                xo = [C.sb(st, [128, D], F32, dma=True, name="xo2") for _ in range(2)]
                m1 = C.sb(st, [128, 512], F32, name="m1")
                m2 = C.sb(st, [128, 512], F32, name="m2")
                mgb = C.sb(st, [128, D], BF16, name="mgb")
                mgT = C.sb(st, [128, 8, 128], BF16, name="mgT")
                xnew = [C.sb(st, [128, D], F32, dma=True, name="xnew") for _ in range(2)]
                sqj = C.sb(st, [128, D], BF16, name="sqj")
                ssf = C.sb(st, [128, 1], F32, name="ssf")
                pb = [C.ps(st, [128, 512], F32, name="pb") for _ in range(3)]
                ptr = C.ps(st, [128, 1024], BF16, name="ptr")
                po = [C.ps(st, [128, 512], F32, name="po") for _ in range(2)]
                for i in range(NSLOT):
                    par = i % 2
                    col = slice(i * 128, (i + 1) * 128)
                    rows = slice(i * 128, (i + 1) * 128)
                    y_, g_, x_, xn_ = yT[par], sg[par], xo[par], xnew[par]
                    C.load(y_.t[:], y_scr[:, :, :, col], y_, [sres("y", i)])
                    C.load(g_.t[:], sg_scr[rows, :], g_, [sres("sg", i)])
                    C.load(x_.t[:], xo_scr[rows, :], x_, [sres("xo", i // 2)])
                    for half in range(2):
                        hs = slice(half * 512, (half + 1) * 512)
                        for br in range(3):
                            for kc in range(4):
                                C.mm(pb[br].t[:], y_.t[:, br, kc, :], wb.t[:, br, kc, hs], kc == 0, kc == 3,
                                     [y_, wb], [pb[br]])
                        C.tt("dve", m1.t[:], pb[0].t[:], g_.t[:, half * 512:half * 512 + 512], ALU.mult, [pb[0], g_], [m1])
                        C.tt("dve", m2.t[:], pb[1].t[:], g_.t[:, 1024 + half * 512:1024 + half * 512 + 512], ALU.mult,
                             [pb[1], g_], [m2])
                        C.tt("pool", m1.t[:], m1.t[:], m2.t[:], ALU.add, [m1, m2], [m1])
                        C.tt("dve", m2.t[:], pb[2].t[:], g_.t[:, 2048 + half * 512:2048 + half * 512 + 512], ALU.mult,
                             [pb[2], g_], [m2])
                        C.tt("pool", mgb.t[:, hs], m1.t[:], m2.t[:], ALU.add, [m1, m2], [mgb])
                    for kc in range(8):
                        C.tr(ptr.t[:, kc * 128:(kc + 1) * 128], mgb.t[:, kc * 128:(kc + 1) * 128], ident.t[:],
                             [mgb, ident], [ptr])
                    C.cp("act", mgT.t[:], ptr.t[:].rearrange("p (a t) -> p a t", a=8), [ptr], [mgT])
                    for half in range(2):
                        hs = slice(half * 512, (half + 1) * 512)
                        for kc in range(8):
                            C.mm(po[half].t[:], mgT.t[:, kc, :], wo.t[:, kc, hs], kc == 0, kc == 7, [mgT, wo], [po[half]])
                        C.tt("dve", xn_.t[:, hs], po[half].t[:], x_.t[:, hs], ALU.add, [po[half], x_], [xn_])
                    if final:
                        C.act(sqj.t[:], xn_.t[:], AF.Square, [xn_], [sqj, ssf], accum_out=ssf.t[:])
                        C.act(ssf.t[:], ssf.t[:], AF.Sqrt, [ssf], [ssf], scale=1.0 / D, bias=EPS)
                        C.recip(ssf.t[:], ssf.t[:], [ssf], [ssf])
                        C.stt(xn_.t[:], xn_.t[:], ssf.t[:, 0:1], fg.t[:], ALU.mult, ALU.mult, [xn_, ssf, fg], [xn_])
                    C.store(dst_rows(i), xn_.t[:], xn_, [sres("out", i)])
                S.barrier()
                S.emit()
                C.end_phase()

        emit_pass(xfull, runK=True, final=False, dst_rows=lambda i: x1full[(2 * i) * 128:(2 * i + 1) * 128, :],
                  **LW[0], **TAB["0"])
        emit_pass(xfull, runK=False, final=False, dst_rows=lambda i: x1full[(2 * i + 1) * 128:(2 * i + 2) * 128, :],
                  **LW[0], **TAB["1"])
        emit_pass(x1full, runK=True, final=True, dst_rows=lambda i: out_d[i * 128:(i + 1) * 128, :],
                  **LW[1], **TAB["o"])
        S.barrier()
        S.emit()
    return nc


_PROG = {}


def _get_prog():
    if "p" not in _PROG:
        _PROG["p"] = build_program()
    return _PROG["p"]


def _maps(inp):
    consts = [_tables(0), _tables(1)]
    maps = []
    for c in range(8):
        b, j = c // 2, c % 2
        m = {
            "xfull": np.ascontiguousarray(inp["x"][b]),
            "mem": np.ascontiguousarray(inp["mem"][b]),
            "relb": np.ascontiguousarray(inp["rel_bias"].reshape(1, 256)),
            "fin_g": np.ascontiguousarray(inp["final_g"].reshape(1, D)),
            "ident": np.eye(128).astype(ml_dtypes.bfloat16),
        }
        for l in range(2):
            m["w_in%d" % l] = np.ascontiguousarray(inp["w_in"][l])
            m["norm_g%d" % l] = np.ascontiguousarray(inp["norm_g"][l].reshape(8, 128).T)
            m["pool_w%d" % l] = np.ascontiguousarray(inp["pool_w"][l])
            m["pool_scale%d" % l] = np.ascontiguousarray(inp["pool_scale"][l].reshape(4, 128).T)
            m["mem_g%d" % l] = np.ascontiguousarray(inp["mem_norm_g"][l].reshape(8, 128).T)
            m["w_mkv%d" % l] = np.ascontiguousarray(inp["w_mem_kv"][l])
            m["w_br%d" % l] = np.ascontiguousarray(inp["w_branch"][l])
            m["w_out%d" % l] = np.ascontiguousarray(inp["w_out"][l])
        for tb, p in (("0", 0), ("1", 1), ("o", j)):
            ind, pen, invc = consts[p]
            blend = np.zeros((128, 2), np.float32)
            blend[:, p] = 1.0
            m["ind" + tb], m["pen" + tb], m["invc" + tb], m["blend" + tb] = ind, pen, invc, blend
        maps.append(m)
    return maps


def _assemble(results):
    full = np.empty((4, NB, 128, D), np.float32)
    for b in range(4):
        for j in range(2):
            full[b, j::2] = results[2 * b + j]["out"].reshape(NSLOT, 128, D)
    return full.reshape(4, NT, D)


def kernel(x, mem, norm_g, w_in, pool_w, pool_scale, mem_norm_g, w_mem_kv, w_branch, w_out, rel_bias, final_g):
    inp = {k: np.asarray(v, dtype=np.float32) for k, v in dict(
        x=x, mem=mem, norm_g=norm_g, w_in=w_in, pool_w=pool_w, pool_scale=pool_scale, mem_norm_g=mem_norm_g,
        w_mem_kv=w_mem_kv, w_branch=w_branch, w_out=w_out, rel_bias=rel_bias, final_g=final_g).items()}
    nc = _get_prog()
    res = run_bass_kernel_spmd(nc, _maps(inp), core_ids=list(range(8)))
    return _assemble(res.results)
```
